# Optimizing a Trainium2 kernel written in Bass

```python
import jax, jax.numpy as jnp
from jax import lax
import numpy as np

D_MODEL = 1024
BATCH = 8
SEQ = 4096
DEPTH = 2

N_A = DEPTH // 2
N_B = DEPTH - N_A
PLE_DIM = 256
FOX_HEADS = 16
FOX_HEAD_DIM = D_MODEL // FOX_HEADS
FOX_IN = 3 * D_MODEL + FOX_HEADS
FOX_GATE_BIAS = 3.0
MLA_HEADS = 16
QK_NOPE_DIM = 128
QK_ROPE_DIM = 64
V_HEAD_DIM = 128
Q_LORA_RANK = 384
KV_LORA_RANK = 256
ROPE_THETA = 10000.0
D_FF = 2816
BLOCK_Q = 128
LN_EPS = 1e-5
RMS_EPS = 1e-6
ALPHA = (2 * DEPTH) ** 0.25
BETA = (8 * DEPTH) ** -0.25

kernel_name = 'yoco_fox_mla_macaron_deepnorm'


def layer_norm(x, g, b):
    xf = x.astype(jnp.float32)
    mu = jnp.mean(xf, axis=-1, keepdims=True)
    xc = xf - mu
    var = jnp.mean(xc * xc, axis=-1, keepdims=True)
    y = xc * lax.rsqrt(var + LN_EPS) * g.astype(jnp.float32) + b.astype(jnp.float32)
    return y.astype(x.dtype)


def rms_norm(x, g):
    xf = x.astype(jnp.float32)
    y = xf * lax.rsqrt(jnp.mean(xf * xf, axis=-1, keepdims=True) + RMS_EPS) * g.astype(jnp.float32)
    return y.astype(x.dtype)


def post_norm(x, delta, g, b):
    return layer_norm(ALPHA * x + delta, g, b)


def swiglu(x, w_in, w_out):
    h = x @ w_in
    gate, up = h[..., :D_FF], h[..., D_FF:]
    return (jax.nn.silu(gate) * up) @ w_out


def rope_tables(seq_len):
    half = QK_ROPE_DIM // 2
    inv = ROPE_THETA ** (-jnp.arange(half, dtype=jnp.float32) * (2.0 / QK_ROPE_DIM))
    ang = jnp.arange(seq_len, dtype=jnp.float32)[:, None] * inv[None, :]
    return jnp.cos(ang), jnp.sin(ang)


def rope(x, cos, sin):
    half = x.shape[-1] // 2
    xf = x.astype(jnp.float32)
    x1, x2 = xf[..., :half], xf[..., half:]
    return jnp.concatenate([x1 * cos - x2 * sin, x2 * cos + x1 * sin], axis=-1).astype(x.dtype)


def causal_block_attention(score_fn, v):
    B, S, H, Dv = v.shape
    n_blocks = S // BLOCK_Q
    k_pos = jnp.arange(S)

    def one_block(blk):
        start = blk * BLOCK_Q
        s = score_fn(start)
        q_pos = start + jnp.arange(BLOCK_Q)
        s = jnp.where(k_pos[None, :] <= q_pos[:, None], s, -jnp.inf)
        w = jax.nn.softmax(s, axis=-1).astype(v.dtype)
        return jnp.einsum('bhqk,bkhd->bqhd', w, v)

    o = lax.map(one_block, jnp.arange(n_blocks))
    return jnp.moveaxis(o, 0, 1).reshape(B, S, H, Dv)


def fox_mixer(x, w_in, b_f, w_o):
    B, S, _ = x.shape
    h = x @ w_in
    q = h[..., :D_MODEL].reshape(B, S, FOX_HEADS, FOX_HEAD_DIM)
    k = h[..., D_MODEL:2 * D_MODEL].reshape(B, S, FOX_HEADS, FOX_HEAD_DIM)
    v = h[..., 2 * D_MODEL:3 * D_MODEL].reshape(B, S, FOX_HEADS, FOX_HEAD_DIM)
    f_logit = h[..., 3 * D_MODEL:].astype(jnp.float32) + b_f.astype(jnp.float32)
    cum = jnp.cumsum(jax.nn.log_sigmoid(f_logit), axis=1).transpose(0, 2, 1)
    scale = FOX_HEAD_DIM ** -0.5

    def scores(start):
        qb = lax.dynamic_slice_in_dim(q, start, BLOCK_Q, axis=1)
        cb = lax.dynamic_slice_in_dim(cum, start, BLOCK_Q, axis=2)
        s = jnp.einsum('bqhd,bkhd->bhqk', qb, k).astype(jnp.float32) * scale
        return s + cb[:, :, :, None] - cum[:, :, None, :]

    o = causal_block_attention(scores, v)
    return o.reshape(B, S, D_MODEL) @ w_o


def shared_kv(x, w_down, kv_norm, w_up, cos, sin):
    B, S, _ = x.shape
    h = x @ w_down
    c_kv = rms_norm(h[..., :KV_LORA_RANK], kv_norm)
    k_rope = rope(h[..., KV_LORA_RANK:], cos, sin)
    kv = jnp.einsum('bsc,chd->bshd', c_kv, w_up)
    return kv[..., :QK_NOPE_DIM], k_rope, kv[..., QK_NOPE_DIM:]


def mla_mixer(x, w_dq, q_norm, w_uq, w_o, k_nope, k_rope, v, cos, sin):
    B, S, _ = x.shape
    c_q = rms_norm(x @ w_dq, q_norm)
    q = jnp.einsum('bsc,chd->bshd', c_q, w_uq)
    q_nope = q[..., :QK_NOPE_DIM]
    q_rope = rope(q[..., QK_NOPE_DIM:], cos[:, None, :], sin[:, None, :])
    scale = (QK_NOPE_DIM + QK_ROPE_DIM) ** -0.5

    def scores(start):
        qn = lax.dynamic_slice_in_dim(q_nope, start, BLOCK_Q, axis=1)
        qr = lax.dynamic_slice_in_dim(q_rope, start, BLOCK_Q, axis=1)
        s = jnp.einsum('bqhd,bkhd->bhqk', qn, k_nope) + jnp.einsum('bqhr,bkr->bhqk', qr, k_rope)
        return s.astype(jnp.float32) * scale

    o = causal_block_attention(scores, v)
    return o.reshape(B, S, MLA_HEADS * V_HEAD_DIM) @ w_o


def setup_inputs(seed: int = 0) -> dict:
    key = jax.random.key(seed)
    ks = iter(jax.random.split(key, 32))
    f32 = jnp.float32

    def nrm(shape, fan_in, scale=1.0):
        return jax.random.normal(next(ks), shape, f32) * (scale * fan_in ** -0.5)

    def gain(shape):
        return 1.0 + 0.02 * jax.random.normal(next(ks), shape, f32)

    def small(shape, s=0.02):
        return s * jax.random.normal(next(ks), shape, f32)

    x = jax.random.normal(next(ks), (BATCH, SEQ, D_MODEL), f32)
    p = jax.random.normal(next(ks), (DEPTH, BATCH, SEQ, PLE_DIM), f32)
    ffn1_w_in = nrm((DEPTH, D_MODEL, 2 * D_FF), D_MODEL)
    ffn1_w_out = nrm((DEPTH, D_FF, D_MODEL), D_FF, BETA)
    ffn2_w_in = nrm((DEPTH, D_MODEL, 2 * D_FF), D_MODEL)
    ffn2_w_out = nrm((DEPTH, D_FF, D_MODEL), D_FF, BETA)
    ln_g = gain((DEPTH, 4, D_MODEL))
    ln_b = small((DEPTH, 4, D_MODEL))
    ple_w_gate = nrm((DEPTH, D_MODEL, D_MODEL), D_MODEL)
    ple_b_gate = small((DEPTH, D_MODEL))
    ple_w_proj = nrm((DEPTH, PLE_DIM, D_MODEL), PLE_DIM, BETA)
    fox_w_in = nrm((N_A, D_MODEL, FOX_IN), D_MODEL)
    fox_w_in = fox_w_in.at[:, :, 2 * D_MODEL:3 * D_MODEL].multiply(BETA)
    fox_b_f = FOX_GATE_BIAS + 0.5 * jax.random.normal(next(ks), (N_A, FOX_HEADS), f32)
    fox_w_o = nrm((N_A, D_MODEL, D_MODEL), D_MODEL, BETA)
    mla_w_dq = nrm((N_B, D_MODEL, Q_LORA_RANK), D_MODEL)
    mla_q_norm = gain((N_B, Q_LORA_RANK))
    mla_w_uq = nrm((N_B, Q_LORA_RANK, MLA_HEADS, QK_NOPE_DIM + QK_ROPE_DIM), Q_LORA_RANK)
    mla_w_o = nrm((N_B, MLA_HEADS * V_HEAD_DIM, D_MODEL), MLA_HEADS * V_HEAD_DIM, BETA)
    kv_w_down = nrm((D_MODEL, KV_LORA_RANK + QK_ROPE_DIM), D_MODEL)
    kv_norm = gain((KV_LORA_RANK,))
    kv_w_up = nrm((KV_LORA_RANK, MLA_HEADS, QK_NOPE_DIM + V_HEAD_DIM), KV_LORA_RANK)
    kv_w_up = kv_w_up.at[:, :, QK_NOPE_DIM:].multiply(BETA)
    return {'x': x, 'p': p,
            'ffn1_w_in': ffn1_w_in, 'ffn1_w_out': ffn1_w_out,
            'ffn2_w_in': ffn2_w_in, 'ffn2_w_out': ffn2_w_out,
            'ln_g': ln_g, 'ln_b': ln_b,
            'ple_w_gate': ple_w_gate, 'ple_b_gate': ple_b_gate, 'ple_w_proj': ple_w_proj,
            'fox_w_in': fox_w_in, 'fox_b_f': fox_b_f, 'fox_w_o': fox_w_o,
            'mla_w_dq': mla_w_dq, 'mla_q_norm': mla_q_norm, 'mla_w_uq': mla_w_uq, 'mla_w_o': mla_w_o,
            'kv_w_down': kv_w_down, 'kv_norm': kv_norm, 'kv_w_up': kv_w_up}


def reference(x, p, ffn1_w_in, ffn1_w_out, ffn2_w_in, ffn2_w_out, ln_g, ln_b,
              ple_w_gate, ple_b_gate, ple_w_proj, fox_w_in, fox_b_f, fox_w_o,
              mla_w_dq, mla_q_norm, mla_w_uq, mla_w_o, kv_w_down, kv_norm, kv_w_up):
    S = x.shape[1]
    cos, sin = rope_tables(S)
    k_nope = k_rope = v_shared = None
    for i in range(DEPTH):
        if i == N_A:
            k_nope, k_rope, v_shared = shared_kv(x, kv_w_down, kv_norm, kv_w_up, cos, sin)
        x = post_norm(x, 0.5 * swiglu(x, ffn1_w_in[i], ffn1_w_out[i]), ln_g[i, 0], ln_b[i, 0])
        if i < N_A:
            mix = fox_mixer(x, fox_w_in[i], fox_b_f[i], fox_w_o[i])
        else:
            j = i - N_A
            mix = mla_mixer(x, mla_w_dq[j], mla_q_norm[j], mla_w_uq[j], mla_w_o[j],
                            k_nope, k_rope, v_shared, cos, sin)
        x = post_norm(x, mix, ln_g[i, 1], ln_b[i, 1])
        x = post_norm(x, 0.5 * swiglu(x, ffn2_w_in[i], ffn2_w_out[i]), ln_g[i, 2], ln_b[i, 2])
        gate = jax.nn.sigmoid((x @ ple_w_gate[i] + ple_b_gate[i]).astype(jnp.float32)).astype(x.dtype)
        x = post_norm(x, gate * (p[i] @ ple_w_proj[i]), ln_g[i, 3], ln_b[i, 3])
    return x
```

```python
import numpy as np
from contextlib import ExitStack

import concourse.bass as bass
import concourse.mybir as mybir
from concourse.bass_utils import run_bass_kernel_spmd

F32 = mybir.dt.float32
BF16 = mybir.dt.bfloat16
AF = mybir.ActivationFunctionType
ALU = mybir.AluOpType

N_CORES = 8
S = 4096
D = 1024
NT = S // 128
DFF = 2816
NFC = DFF // 128
DEPTH = 2
ALPHA = (2 * DEPTH) ** 0.25
LN_EPS = 1e-5
RMS_EPS = 1e-6
PLE = 256
FOX_H = 16
FOX_D = 64
MLA_H = 16
NOPE = 128
ROPE = 64
VD = 128
QL = 384
KVL = 256


class Tl:
    __slots__ = ("ap", "w", "r", "rd")

    def __init__(self, ap=None):
        self.ap = ap
        self.w = None
        self.r = {}
        self.rd = []


class Op:
    __slots__ = ("eng", "fn", "deps", "signal", "ev", "is_dma", "barrier", "noattach")

    def __init__(self, eng, fn, is_dma):
        self.eng = eng
        self.fn = fn
        self.deps = set()
        self.signal = False
        self.ev = None
        self.is_dma = is_dma
        self.barrier = False
        self.noattach = False


ATTACH_WAITS = False
PREFETCH = True
PRE = [None]
FOX_PAIRS = 8
MLA_DEFER = 3
ATT_LA = 4
FOX_SKIP = ()


class Prog:
    ENGS = ("pe", "act", "dve", "pool", "sp")

    def __init__(self, nc, es):
        self.nc = nc
        self.es = es
        self.nsem = 0
        self.ops = []
        self.emitted = 0
        self.eobj = {"pe": nc.tensor, "act": nc.scalar, "dve": nc.vector, "pool": nc.gpsimd, "sp": nc.sync}
        self.sem = {}
        for e in ("pe", "act", "dve", "pool"):
            self.sem[e] = es.enter_context(nc.semaphore("s_" + e))
        self.cnt = {e: 0 for e in self.sem}
        self.dma_pool = {"sp": [], "pool": [], "act": []}
        for q, n in (("sp", 32), ("pool", 16), ("act", 0)):
            for i in range(n):
                self.dma_pool[q].append([es.enter_context(nc.semaphore("d_%s%d" % (q, i))), 0])
        self.dma_rr = {"sp": 0, "pool": 0, "act": 0}
        self.waited = {e: {} for e in self.ENGS}
        self.last_op = {e: None for e in self.ENGS}
        self.dma_since_barrier = []

    def add(self, eng, fn, reads=(), writes=(), is_dma=False):
        i = len(self.ops)
        op = Op(eng, fn, is_dma)
        ops = self.ops
        deps = op.deps
        for t in reads:
            if t.w is not None:
                deps.add(t.w)
        for t in writes:
            if t.w is not None:
                j = t.w
                if is_dma or ops[j].is_dma or ops[j].eng != eng:
                    deps.add(j)
            for e, j in t.r.items():
                if is_dma or e != eng:
                    deps.add(j)
            for j in t.rd:
                deps.add(j)
        for j in deps:
            ops[j].signal = True
        for t in reads:
            if is_dma:
                t.rd.append(i)
            else:
                t.r[eng] = i
        for t in writes:
            t.w = i
            t.r = {}
            t.rd = []
        if is_dma:
            op.signal = True
            self.dma_since_barrier.append(i)
        self.last_op[eng] = i
        ops.append(op)
        return i

    def barrier(self):
        lasts = [j for j in self.last_op.values() if j is not None]
        for j in lasts:
            self.ops[j].signal = True
        for e in self.ENGS:
            op = Op(e, None, False)
            op.barrier = True
            op.deps = set(lasts) | set(self.dma_since_barrier)
            self.ops.append(op)
        self.dma_since_barrier = []
        self.flush()
        for e in self.sem:
            if self.cnt[e] > 1500:
                self.nsem += 1
                self.sem[e] = self.es.enter_context(self.nc.semaphore("s_%s_%d" % (e, self.nsem)))
                self.cnt[e] = 0

    def _wait(self, eng, sem, val):
        w = self.waited[eng]
        key = sem.num
        if w.get(key, 0) < val:
            self.eobj[eng].wait_ge(sem, val)
            w[key] = val

    def flush(self):
        ops = self.ops
        for i in range(self.emitted, len(ops)):
            op = ops[i]
            need = {}
            for j in op.deps:
                ev = ops[j].ev
                if ev is None:
                    continue
                sem, val = ev
                k = sem.num
                if k not in need or need[k][1] < val:
                    need[k] = (sem, val)
            w = self.waited[op.eng]
            pend = [need[k] for k in sorted(need) if w.get(k, 0) < need[k][1]]
            attach = None
            if pend and ATTACH_WAITS and not op.barrier and not op.is_dma and not op.noattach:
                attach = pend.pop()
            for sem, val in pend:
                self._wait(op.eng, sem, val)
            if op.barrier:
                continue
            if op.is_dma:
                pool = self.dma_pool[op.eng]
                slot = pool[self.dma_rr[op.eng] % len(pool)]
                self.dma_rr[op.eng] += 1
                sw = None
                if slot[1] > 0 and w.get(slot[0].num, 0) < slot[1]:
                    if ATTACH_WAITS:
                        sw = (slot[0], slot[1])
                    else:
                        self._wait(op.eng, slot[0], slot[1])
                ins = op.fn()
                if sw is not None:
                    ins._wait_ge(sw[0], sw[1])
                    w[sw[0].num] = sw[1]
                slot[1] += 16
                ins.then_inc(slot[0], 16)
                op.ev = (slot[0], slot[1])
            else:
                ins = op.fn()
                if attach is not None:
                    ins._wait_ge(attach[0], attach[1])
                    w[attach[0].num] = attach[1]
                if op.signal:
                    self.cnt[op.eng] += 1
                    ins.then_inc(self.sem[op.eng], 1)
                    op.ev = (self.sem[op.eng], self.cnt[op.eng])
            op.fn = None
        self.emitted = len(ops)

    def dma(self, q, out, in_, reads, writes):
        e = self.eobj[q]
        return self.add(q, lambda: e.dma_start(out=out, in_=in_), reads, writes, is_dma=True)

    def mm(self, out, lhsT, rhs, start, stop, reads, writes):
        t = self.nc.tensor
        i = self.add("pe", lambda: t.matmul(out, lhsT, rhs, start=start, stop=stop), reads, writes)
        if lhsT.dtype == F32:
            self.ops[i].noattach = True
        return i

    def tr(self, out, in_, ident, reads, writes):
        t = self.nc.tensor
        return self.add("pe", lambda: t.transpose(out, in_, ident), reads, writes)

    def act(self, out, in_, func, reads, writes, bias=None, scale=None, accum_out=None):
        s = self.nc.scalar
        kw = {}
        if bias is not None:
            kw["bias"] = bias
        if scale is not None:
            kw["scale"] = scale
        if accum_out is not None:
            kw["accum_out"] = accum_out
        return self.add("act", lambda: s.activation(out, in_, func, **kw), reads, writes)

    def v(self, eng, name, reads, writes, *a, **kw):
        e = self.eobj[eng]
        f = getattr(e, name)
        return self.add(eng, lambda: f(*a, **kw), reads, writes)


class Ctx:
    pass


def load_x_transposed(P, C, Xin_t, xl, xb, tp, xT_ap, xT_t):
    nc = P.nc
    P.dma("sp", xl.ap, Xin_t.ap, [Xin_t], [xl])
    P.act(xb.ap, xl.ap, AF.Copy, [xl], [xb])
    for k in range(8):
        P.tr(tp.ap[:, k * 128:(k + 1) * 128], xb.ap[:, k * 128:(k + 1) * 128], C.ident.ap, [xb, C.ident], [tp])
    P.v("dve", "tensor_copy", [tp], [xT_t], xT_ap, tp.ap.rearrange("p (k t) -> p k t", k=8))


def post_norm_tile(P, C, xr, g_bc, b_bc, Xout_t, st, mv, rs, nm):
    eps = LN_EPS / (ALPHA * ALPHA)
    P.v("dve", "bn_stats", [xr], [st], st.ap[:, 0:6], xr.ap[:, 0:512])
    P.v("dve", "bn_stats", [xr], [st], st.ap[:, 6:12], xr.ap[:, 512:1024])
    P.v("dve", "bn_aggr", [st], [mv], mv.ap, st.ap)
    P.act(rs.ap, mv.ap[:, 1:2], AF.Sqrt, [mv], [rs], bias=C.eps_ln.ap, scale=1.0)
    P.v("dve", "reciprocal", [rs], [rs], rs.ap, rs.ap)
    P.v("dve", "scalar_tensor_tensor", [mv, rs], [nm], nm.ap, mv.ap[:, 0:1], -1.0, rs.ap, ALU.mult, ALU.mult)
    P.act(xr.ap, xr.ap, AF.Identity, [xr, rs, nm], [xr], bias=nm.ap, scale=rs.ap)
    P.v("pool", "tensor_tensor", [xr, g_bc], [xr], xr.ap, xr.ap, g_bc.ap, ALU.mult)
    P.v("pool", "tensor_tensor", [xr, b_bc], [xr], xr.ap, xr.ap, b_bc.ap, ALU.add)
    P.dma("pool", Xout_t.ap, xr.ap, [xr], [Xout_t])


def load_bcast(P, es, name, dram_row_ap, n):
    nc = P.nc
    t = Tl(sb(nc, es, name, [128, n], F32)[:, :])
    P.dma("sp", t.ap, dram_row_ap.partition_broadcast(128), [], [t])
    return t


_uid = [0]


def sb(nc, es, name, shape, dt):
    _uid[0] += 1
    return es.enter_context(nc.sbuf_tensor("%s_%d" % (name, _uid[0]), shape, dt))


def ps(nc, es, name, shape, dt):
    _uid[0] += 1
    return es.enter_context(nc.psum_tensor("%s_%d" % (name, _uid[0]), shape, dt))


NPRE = 5


def ffn_prefetch(P, es, w_in_d, tag):
    nc = P.nc
    wpre = sb(nc, es, "wpre" + tag, [128, 8, 2 * NPRE * 256], BF16)
    w_in_v = w_in_d.rearrange("(k p) n -> p k n", p=128)
    tl = {}
    for gi in range(NPRE):
        for half in range(2):
            c0 = half * DFF + gi * 256
            o0 = (half * NPRE + gi) * 256
            t = Tl(wpre[:, :, o0:o0 + 256])
            P.dma("pool", t.ap, w_in_v[:, :, c0:c0 + 256], [], [t])
            tl[(half, gi)] = (t, o0)
    return wpre, tl


def phase_ffn(P, C, Xin, Xout, w_in_d, w_out_d, g_d, b_d, tag, pre=None):
    nc = P.nc
    with ExitStack() as es:
        npre = NPRE if pre is not None else 0
        ngr = NFC // 2 - npre
        w_in = sb(nc, es, "w_in" + tag, [128, 8, 2 * ngr * 256], BF16)
        w_out = sb(nc, es, "w_out" + tag, [128, NFC, D], BF16)
        w_in_v = w_in_d.rearrange("(k p) n -> p k n", p=128)
        win_t = {}
        wsrc = {}
        for gi in range(NFC // 2):
            for half in range(2):
                c0 = half * DFF + gi * 256
                if gi < npre:
                    t, o0 = pre[1][(half, gi)]
                    wsrc[(half, gi)] = (pre[0], o0 - c0)
                else:
                    o0 = (half * ngr + gi - npre) * 256
                    t = Tl(w_in[:, :, o0:o0 + 256])
                    P.dma("pool", t.ap, w_in_v[:, :, c0:c0 + 256], [], [t])
                    wsrc[(half, gi)] = (w_in, o0 - c0)
                win_t[(half, gi)] = t
        w_out_v = w_out_d.rearrange("(c p) n -> p c n", p=128)
        wout_t = []
        for gi in range(NFC // 2):
            t = Tl(w_out[:, 2 * gi:2 * gi + 2, :])
            P.dma("pool", t.ap, w_out_v[:, 2 * gi:2 * gi + 2, :], [], [t])
            wout_t.append(t)
        g_bc = load_bcast(P, es, "g_bc" + tag, g_d, D)
        b_bc = load_bcast(P, es, "b_bc" + tag, b_d, D)

        xl = [Tl(sb(nc, es, "xl%d%s" % (i, tag), [128, D], F32)[:, :]) for i in range(2)]
        xb = [Tl(sb(nc, es, "xb%d%s" % (i, tag), [128, D], BF16)[:, :]) for i in range(2)]
        xr = [Tl(sb(nc, es, "xr%d%s" % (i, tag), [128, D], F32)[:, :]) for i in range(2)]
        xT = [sb(nc, es, "xT%d%s" % (i, tag), [128, 8, 512], BF16) for i in range(2)]
        xT_t = [[Tl(xT[i][:, :, t * 128:(t + 1) * 128]) for t in range(4)] for i in range(2)]
        actT = sb(nc, es, "actT" + tag, [128, NFC, 512], BF16)
        act_t = [Tl(actT[:, c, :]) for c in range(NFC)]
        sg = [Tl(sb(nc, es, "sg%d%s" % (i, tag), [128, 512], F32)[:, :]) for i in range(2)]
        st = [Tl(sb(nc, es, "st%d%s" % (i, tag), [128, 12], F32)[:, :]) for i in range(2)]
        mv = [Tl(sb(nc, es, "mv%d%s" % (i, tag), [128, 2], F32)[:, :]) for i in range(2)]
        rs = [Tl(sb(nc, es, "rs%d%s" % (i, tag), [128, 1], F32)[:, :]) for i in range(2)]
        nm = [Tl(sb(nc, es, "nm%d%s" % (i, tag), [128, 1], F32)[:, :]) for i in range(2)]
        tp = [Tl(ps(nc, es, "tp%d%s" % (i, tag), [128, D], BF16)[:, :]) for i in range(2)]
        hg = [Tl(ps(nc, es, "hg%d%s" % (i, tag), [128, 512], F32)[:, :]) for i in range(2)]
        hu = [Tl(ps(nc, es, "hu%d%s" % (i, tag), [128, 512], F32)[:, :]) for i in range(2)]
        po = [Tl(ps(nc, es, "po%d%s" % (i, tag), [128, 512], F32)[:, :]) for i in range(2)]

        cfac = 0.5 / ALPHA
        nblk = NT // 4
        def prep_x(bb, t):
            ti = 4 * bb + t
            load_x_transposed(P, C, Xin[ti], xl[ti % 2], xb[ti % 2], tp[ti % 2],
                              xT[bb % 2][:, :, t * 128:(t + 1) * 128], xT_t[bb % 2][t])

        for t in range(4):
            prep_x(0, t)
        for b in range(nblk):
            sbk = b % 2
            for c in range(NFC):
                if b + 1 < nblk and c in (2, 7, 12, 17):
                    prep_x(b + 1, (c - 2) // 5)
                s2 = c % 2
                for (half, acc) in ((0, hg[s2]), (1, hu[s2])):
                    wt = win_t[(half, c // 2)]
                    wten, woff = wsrc[(half, c // 2)]
                    col = half * DFF + c * 128 + woff
                    for k in range(8):
                        P.mm(acc.ap, wten[:, k, col:col + 128], xT[sbk][:, k, :], k == 0, k == 7,
                             [wt] + xT_t[sbk], [acc])
                P.act(sg[s2].ap, hg[s2].ap, AF.Silu, [hg[s2]], [sg[s2]])
                P.v("dve", "tensor_tensor", [sg[s2], hu[s2]], [act_t[c]], act_t[c].ap, sg[s2].ap, hu[s2].ap, ALU.mult)
            for t in range(4):
                ti = 4 * b + t
                r = xr[ti % 2]
                P.dma("sp", r.ap, Xin[ti].ap, [Xin[ti]], [r])
                for n in range(2):
                    acc = po[n]
                    for c in range(NFC):
                        P.mm(acc.ap, actT[:, c, t * 128:(t + 1) * 128], w_out[:, c, n * 512:(n + 1) * 512],
                             c == 0, c == NFC - 1, [act_t[c], wout_t[c // 2]], [acc])
                    P.v("dve", "scalar_tensor_tensor", [acc, r], [r], r.ap[:, n * 512:(n + 1) * 512], acc.ap, cfac,
                        r.ap[:, n * 512:(n + 1) * 512], ALU.mult, ALU.add)
                post_norm_tile(P, C, r, g_bc, b_bc, Xout[ti], st[ti % 2], mv[ti % 2], rs[ti % 2], nm[ti % 2])
        P.barrier()


def phase_ple(P, C, Xin, Xout, Pt, wg_d, bg_d, wp_d, g_d, b_d, tag):
    nc = P.nc
    with ExitStack() as es:
        wg = Tl(sb(nc, es, "wg" + tag, [128, 8, D], BF16)[:, :, :])
        wp = Tl(sb(nc, es, "wp" + tag, [128, 2, D], BF16)[:, :, :])
        P.dma("pool", wg.ap, wg_d.rearrange("(k p) n -> p k n", p=128), [], [wg])
        P.dma("pool", wp.ap, wp_d.rearrange("(k p) n -> p k n", p=128), [], [wp])
        bg_bc = load_bcast(P, es, "bg_bc" + tag, bg_d, D)
        g_bc = load_bcast(P, es, "g_bc" + tag, g_d, D)
        b_bc = load_bcast(P, es, "b_bc" + tag, b_d, D)
        xl = [Tl(sb(nc, es, "xl%d%s" % (i, tag), [128, D], F32)[:, :]) for i in range(2)]
        xb = [Tl(sb(nc, es, "xb%d%s" % (i, tag), [128, D], BF16)[:, :]) for i in range(2)]
        pl = [Tl(sb(nc, es, "pl%d%s" % (i, tag), [128, PLE], F32)[:, :]) for i in range(2)]
        pbf = [Tl(sb(nc, es, "pbf%d%s" % (i, tag), [128, PLE], BF16)[:, :]) for i in range(2)]
        xr = [Tl(sb(nc, es, "xr%d%s" % (i, tag), [128, D], F32)[:, :]) for i in range(4)]
        xT = [sb(nc, es, "xT%d%s" % (i, tag), [128, 8, 512], BF16) for i in range(2)]
        xT_t = [[Tl(xT[i][:, :, t * 128:(t + 1) * 128]) for t in range(4)] for i in range(2)]
        pT = [sb(nc, es, "pT%d%s" % (i, tag), [128, 2, 512], BF16) for i in range(2)]
        pT_t = [[Tl(pT[i][:, :, t * 128:(t + 1) * 128]) for t in range(4)] for i in range(2)]
        tmp = [Tl(sb(nc, es, "tmp%d%s" % (i, tag), [128, 512], F32)[:, :]) for i in range(2)]
        st = [Tl(sb(nc, es, "st%d%s" % (i, tag), [128, 12], F32)[:, :]) for i in range(2)]
        mv = [Tl(sb(nc, es, "mv%d%s" % (i, tag), [128, 2], F32)[:, :]) for i in range(2)]
        rs = [Tl(sb(nc, es, "rs%d%s" % (i, tag), [128, 1], F32)[:, :]) for i in range(2)]
        nm = [Tl(sb(nc, es, "nm%d%s" % (i, tag), [128, 1], F32)[:, :]) for i in range(2)]
        tp = [Tl(ps(nc, es, "tp%d%s" % (i, tag), [128, D], BF16)[:, :]) for i in range(2)]
        tpp = Tl(ps(nc, es, "tpp" + tag, [128, PLE], BF16)[:, :])
        pg = [Tl(ps(nc, es, "pg%d%s" % (i, tag), [128, 512], F32)[:, :]) for i in range(2)]
        pp = [Tl(ps(nc, es, "pp%d%s" % (i, tag), [128, 512], F32)[:, :]) for i in range(2)]
        def prep(bb, t):
            ti = 4 * bb + t
            sk = bb % 2
            load_x_transposed(P, C, Xin[ti], xl[ti % 2], xb[ti % 2], tp[ti % 2],
                              xT[sk][:, :, t * 128:(t + 1) * 128], xT_t[sk][t])
            P.dma("sp", pl[ti % 2].ap, Pt[ti].ap, [Pt[ti]], [pl[ti % 2]])
            P.act(pbf[ti % 2].ap, pl[ti % 2].ap, AF.Copy, [pl[ti % 2]], [pbf[ti % 2]])
            for k in range(2):
                P.tr(tpp.ap[:, k * 128:(k + 1) * 128], pbf[ti % 2].ap[:, k * 128:(k + 1) * 128], C.ident.ap,
                     [pbf[ti % 2], C.ident], [tpp])
            P.v("dve", "tensor_copy", [tpp], [pT_t[sk][t]], pT[sk][:, :, t * 128:(t + 1) * 128],
                tpp.ap.rearrange("p (k t) -> p k t", k=2))

        for t in range(4):
            prep(0, t)
        for b in range(NT // 4):
            sbk = b % 2
            for t in range(4):
                ti = 4 * b + t
                r = xr[t]
                if b + 1 < NT // 4:
                    prep(b + 1, t)
                P.dma("sp", r.ap, Xin[ti].ap, [Xin[ti]], [r])
                for n in range(2):
                    cs = slice(n * 512, (n + 1) * 512)
                    for k in range(8):
                        P.mm(pg[n].ap, xT[sbk][:, k, t * 128:(t + 1) * 128], wg.ap[:, k, cs], k == 0, k == 7,
                             [xT_t[sbk][t], wg], [pg[n]])
                    for k in range(2):
                        P.mm(pp[n].ap, pT[sbk][:, k, t * 128:(t + 1) * 128], wp.ap[:, k, cs], k == 0, k == 1,
                             [pT_t[sbk][t], wp], [pp[n]])
                    tm = tmp[n]
                    P.v("dve", "tensor_tensor", [pg[n], bg_bc], [tm], tm.ap, pg[n].ap, bg_bc.ap[:, cs], ALU.add)
                    P.act(tm.ap, tm.ap, AF.Sigmoid, [tm], [tm])
                    P.v("dve", "tensor_tensor", [tm, pp[n]], [tm], tm.ap, tm.ap, pp[n].ap, ALU.mult)
                    P.v("dve", "scalar_tensor_tensor", [tm, r], [r], r.ap[:, cs], tm.ap, 1.0 / ALPHA, r.ap[:, cs],
                        ALU.mult, ALU.add)
            for t in range(4):
                ti = 4 * b + t
                post_norm_tile(P, C, xr[t], g_bc, b_bc, Xout[ti], st[ti % 2], mv[ti % 2], rs[ti % 2], nm[ti % 2])
        P.barrier()


def phase_attout(P, C, Xin, Xout, OTv, OT_reads, nch, wo_d, g_d, b_d, tag):
    nc = P.nc
    ng = 4
    cpg = nch // ng
    with ExitStack() as es:
        wo = sb(nc, es, "wo" + tag, [128, nch, D], BF16)
        wo_v = wo_d.rearrange("(c d) n -> d c n", d=128)
        wo_t = []
        for q in range(ng):
            t = Tl(wo[:, q * cpg:(q + 1) * cpg, :])
            P.dma("pool", t.ap, wo_v[:, q * cpg:(q + 1) * cpg, :], [], [t])
            wo_t.append(t)
        g_bc = load_bcast(P, es, "g_bc" + tag, g_d, D)
        b_bc = load_bcast(P, es, "b_bc" + tag, b_d, D)
        ob = [sb(nc, es, "ob%d%s" % (i, tag), [128, nch, 512], BF16) for i in range(2)]
        ob_t = [[Tl(ob[i][:, q * cpg:(q + 1) * cpg, :]) for q in range(ng)] for i in range(2)]
        xr = [Tl(sb(nc, es, "xr%d%s" % (i, tag), [128, D], F32)[:, :]) for i in range(4)]
        st = [Tl(sb(nc, es, "st%d%s" % (i, tag), [128, 12], F32)[:, :]) for i in range(2)]
        mv = [Tl(sb(nc, es, "mv%d%s" % (i, tag), [128, 2], F32)[:, :]) for i in range(2)]
        rs = [Tl(sb(nc, es, "rs%d%s" % (i, tag), [128, 1], F32)[:, :]) for i in range(2)]
        nm = [Tl(sb(nc, es, "nm%d%s" % (i, tag), [128, 1], F32)[:, :]) for i in range(2)]
        po = [Tl(ps(nc, es, "po%d%s" % (i, tag), [128, 512], F32)[:, :]) for i in range(4)]
        def load_ob(bb):
            for q in range(ng):
                rd = []
                for c in range(q * cpg, (q + 1) * cpg):
                    rd += OT_reads(c, bb)
                P.dma("sp", ob_t[bb % 2][q].ap,
                      OTv[q * cpg:(q + 1) * cpg, :, bb * 512:(bb + 1) * 512].rearrange("c d t -> d c t"), rd, [ob_t[bb % 2][q]])

        load_ob(0)
        for b in range(NT // 4):
            sl = b % 2
            if b + 1 < NT // 4:
                load_ob(b + 1)
            for t in range(4):
                ti = 4 * b + t
                r = xr[ti % 4]
                P.dma("sp", r.ap, Xin[ti].ap, [Xin[ti]], [r])
                for n in range(2):
                    cs = slice(n * 512, (n + 1) * 512)
                    acc = po[(2 * ti + n) % 4]
                    for c in range(nch):
                        P.mm(acc.ap, ob[sl][:, c, t * 128:(t + 1) * 128], wo[:, c, cs], c == 0, c == nch - 1,
                             [ob_t[sl][c // cpg], wo_t[c // cpg]], [acc])
                    P.v("dve", "scalar_tensor_tensor", [acc, r], [r], r.ap[:, cs], acc.ap, 1.0 / ALPHA, r.ap[:, cs],
                        ALU.mult, ALU.add)
                post_norm_tile(P, C, r, g_bc, b_bc, Xout[ti], st[ti % 2], mv[ti % 2], rs[ti % 2], nm[ti % 2])
        P.barrier()


def phase_latent(P, C, Xin, w_d, ncol, gn_d, LT, LT_t, tag, rope_w_d=None, KRT=None, KRT_t=None):
    nc = P.nc
    nch = ncol // 128
    with ExitStack() as es:
        w = Tl(sb(nc, es, "wl" + tag, [128, 8, ncol], BF16)[:, :, :])
        P.dma("pool", w.ap, w_d.rearrange("(k p) n -> p k n", p=128)[:, :, 0:ncol], [], [w])
        gn_bc = load_bcast(P, es, "gn_bc" + tag, gn_d, ncol)
        eps_r = Tl(sb(nc, es, "eps_r" + tag, [128, 1], F32)[:, :])
        P.v("dve", "memset", [], [eps_r], eps_r.ap, RMS_EPS)
        if rope_w_d is not None:
            phase_consts(P, C, es, ["cosT", "sinT"], tag)
            wr = Tl(sb(nc, es, "wr" + tag, [128, 8, 64], BF16)[:, :, :])
            wrs = Tl(sb(nc, es, "wrs" + tag, [128, 8, 64], BF16)[:, :, :])
            rv = rope_w_d.rearrange("(k p) n -> p k n", p=128)
            P.dma("pool", wr.ap, rv[:, :, ncol:ncol + 64], [], [wr])
            P.dma("pool", wrs.ap[:, :, 0:32], rv[:, :, ncol + 32:ncol + 64], [], [wrs])
            P.dma("pool", wrs.ap[:, :, 32:64], rv[:, :, ncol:ncol + 32], [], [wrs])
            pa = Tl(ps(nc, es, "pa" + tag, [64, 512], F32)[:, :])
            pb_ = Tl(ps(nc, es, "pb" + tag, [64, 512], F32)[:, :])
            t1 = Tl(sb(nc, es, "t1" + tag, [64, 512], F32)[:, :])
            t2 = Tl(sb(nc, es, "t2" + tag, [64, 512], F32)[:, :])
            kr = [Tl(sb(nc, es, "kr%d%s" % (i, tag), [64, 512], BF16)[:, :]) for i in range(2)]
        xl = [Tl(sb(nc, es, "xl%d%s" % (i, tag), [128, D], F32)[:, :]) for i in range(2)]
        xb = [Tl(sb(nc, es, "xb%d%s" % (i, tag), [128, D], BF16)[:, :]) for i in range(2)]
        xT = [sb(nc, es, "xT%d%s" % (i, tag), [128, 8, 512], BF16) for i in range(2)]
        xT_t = [[Tl(xT[i][:, :, t * 128:(t + 1) * 128]) for t in range(4)] for i in range(2)]
        junk = Tl(sb(nc, es, "junk" + tag, [128, ncol], F32)[:, :])
        ss = [Tl(sb(nc, es, "ss%d%s" % (i, tag), [128, 1], F32)[:, :]) for i in range(2)]
        rs = [Tl(sb(nc, es, "rs%d%s" % (i, tag), [128, 1], F32)[:, :]) for i in range(2)]
        cb = [Tl(sb(nc, es, "cb%d%s" % (i, tag), [128, ncol], BF16)[:, :]) for i in range(2)]
        lt = [sb(nc, es, "lt%d%s" % (i, tag), [128, nch, 512], BF16) for i in range(2)]
        lt_t = [[Tl(lt[i][:, :, t * 128:(t + 1) * 128]) for t in range(4)] for i in range(2)]
        tp = [Tl(ps(nc, es, "tp%d%s" % (i, tag), [128, D], BF16)[:, :]) for i in range(2)]
        hk = [Tl(ps(nc, es, "hk%d%s" % (i, tag), [128, 512], F32)[:, :]) for i in range(2)]
        tp2 = Tl(ps(nc, es, "tp2" + tag, [128, ncol], BF16)[:, :])
        def prep(bb, t):
            ti = 4 * bb + t
            load_x_transposed(P, C, Xin[ti], xl[ti % 2], xb[ti % 2], tp[ti % 2],
                              xT[bb % 2][:, :, t * 128:(t + 1) * 128], xT_t[bb % 2][t])

        for t in range(4):
            prep(0, t)
        for b in range(NT // 4):
            sbk = b % 2
            for t in range(4):
                ti = 4 * b + t
                if b + 1 < NT // 4:
                    prep(b + 1, t)
                h = hk[ti % 2]
                for k in range(8):
                    P.mm(h.ap[:, 0:ncol], xT[sbk][:, k, t * 128:(t + 1) * 128], w.ap[:, k, :], k == 0, k == 7,
                         [xT_t[sbk][t], w], [h])
                s_ = ss[ti % 2]
                r_ = rs[ti % 2]
                P.v("dve", "memset", [], [s_], s_.ap, 0.0)
                P.act(junk.ap, h.ap[:, 0:ncol], AF.Square, [h, s_], [junk, s_], accum_out=s_.ap)
                P.act(r_.ap, s_.ap, AF.Sqrt, [s_, eps_r], [r_], bias=eps_r.ap, scale=1.0 / ncol)
                P.v("dve", "reciprocal", [r_], [r_], r_.ap, r_.ap)
                c_ = cb[ti % 2]
                P.v("dve", "scalar_tensor_tensor", [h, r_, gn_bc], [c_], c_.ap, h.ap[:, 0:ncol], r_.ap, gn_bc.ap,
                    ALU.mult, ALU.mult)
                for k in range(nch):
                    P.tr(tp2.ap[:, k * 128:(k + 1) * 128], c_.ap[:, k * 128:(k + 1) * 128], C.ident.ap,
                         [c_, C.ident], [tp2])
                P.v("dve", "tensor_copy", [tp2], [lt_t[sbk][t]], lt[sbk][:, :, t * 128:(t + 1) * 128],
                    tp2.ap.rearrange("p (k t) -> p k t", k=nch))
            P.dma("pool", LT[:, :, b * 512:(b + 1) * 512].rearrange("c p t -> p c t"), lt[sbk][:, :, :], lt_t[sbk], [LT_t[b]])
            if rope_w_d is not None:
                cs = slice(b * 512, (b + 1) * 512)
                for k in range(8):
                    P.mm(pa.ap, wr.ap[:, k, :], xT[sbk][:, k, :], k == 0, k == 7, [wr] + xT_t[sbk], [pa])
                for k in range(8):
                    P.mm(pb_.ap, wrs.ap[:, k, :], xT[sbk][:, k, :], k == 0, k == 7, [wrs] + xT_t[sbk], [pb_])
                P.v("dve", "tensor_tensor", [pa, C.cosT], [t1], t1.ap, pa.ap, C.cosT.ap[:, cs], ALU.mult)
                P.v("dve", "tensor_tensor", [pb_, C.sinT], [t2], t2.ap, pb_.ap, C.sinT.ap[:, cs], ALU.mult)
                P.v("dve", "tensor_tensor", [t1, t2], [kr[sbk]], kr[sbk].ap, t1.ap, t2.ap, ALU.add)
                P.dma("pool", KRT[:, cs], kr[sbk].ap, [kr[sbk]], [KRT_t[b]])
        P.barrier()


def load_const(P, es, name, d_ap, shape, dt):
    h = sb(P.nc, es, name, shape, dt)
    t = Tl(h[tuple(slice(None) for _ in shape)])
    P.dma("sp", t.ap, d_ap, [], [t])
    return t


def phase_consts(P, C, es, names, tag):
    for n in names:
        d_ap, shape, dt = C.d[n]
        setattr(C, n, load_const(P, es, "k_" + n + tag, d_ap, shape, dt))


def drive_attention(tiles, LA, stageA, stageB, epilogue, items_for, n_units, blocks_per_unit, defer=6):
    for it in items_for(0):
        it()
    n_in_unit = {}
    for t in tiles:
        t["k"] = n_in_unit.get(t["u"], 0)
        n_in_unit[t["u"]] = t["k"] + 1
    pending, total, popped = {}, {}, {}
    deferred = []
    n = len(tiles)
    for i in range(n + LA):
        if i < n:
            stageA(i, tiles[i])
        j = i - LA
        if j < 0:
            continue
        t = tiles[j]
        stageB(j, t)
        if t["last"]:
            d = epilogue(t)
            if d is not None:
                deferred.append((j + defer, d))
        while deferred and deferred[0][0] <= j:
            deferred.pop(0)[1]()
        u = t["u"]
        if u + 1 < n_units:
            if u + 1 not in pending:
                pending[u + 1] = items_for(u + 1)
                total[u + 1] = len(pending[u + 1])
                popped[u + 1] = 0
            q = pending[u + 1]
            target = min(total[u + 1], int((t["k"] + 1) * total[u + 1] / (0.85 * n_in_unit[u])) + 1)
            while popped[u + 1] < target and q:
                q.pop(0)()
                popped[u + 1] += 1
    for _, d in deferred:
        d()


def diag_range(kt, b):
    i = kt - 4 * b
    c0 = max(i, 0) * 128
    return i, c0


def phase_fox(P, C, Xin, w_in_d, bf_d, OT, OT_t, tag="fx"):
    nc = P.nc
    scale = FOX_D ** -0.5
    w_v = w_in_d.rearrange("(k p) n -> p k n", p=128)
    with ExitStack() as es:
        phase_consts(P, C, es, ["tri", "Lf", "E127f", "E127b", "sel", "onesf"], tag)
        xTa = sb(nc, es, "xTa", [128, 8, S], BF16)
        xTa_t = [Tl(xTa[:, :, t * 128:(t + 1) * 128]) for t in range(NT)]
        wf = Tl(sb(nc, es, "wf", [128, 8, 16], BF16)[:, :, :])
        P.dma("pool", wf.ap, w_v[:, :, 3 * D:3 * D + 16], [], [wf])
        bf_bc = load_bcast(P, es, "bf_bc", bf_d, 16)
        logf = sb(nc, es, "logf", [128, NT, 16], F32)
        logf_t = [Tl(logf[:, t, :]) for t in range(NT)]
        cum = sb(nc, es, "cum", [128, NT, 16], F32)
        cum_t = [Tl(cum[:, t, :]) for t in range(NT)]
        ncum = sb(nc, es, "ncum", [128, NT, 16], F32)
        ncum_t = [Tl(ncum[:, t, :]) for t in range(NT)]
        cumb = sb(nc, es, "cumb", [128, NT, 16], BF16)
        cumb_t = [Tl(cumb[:, t, :]) for t in range(NT)]
        Cq = sb(nc, es, "Cq", [16, S], BF16)
        Cq_t = [Tl(Cq[:, g * 512:(g + 1) * 512]) for g in range(8)]
        wpair = [sb(nc, es, "wpair%d" % i, [128, 8, 384], BF16) for i in range(2)]
        wpair_t = [Tl(wpair[i][:, :, :]) for i in range(2)]
        qT = [sb(nc, es, "qT%d" % i, [128, 2, S], BF16) for i in range(2)]
        kT = [sb(nc, es, "kT%d" % i, [128, S], BF16) for i in range(2)]
        vS = [sb(nc, es, "vS%d" % i, [128, NT, 2, 65], BF16) for i in range(2)]
        qT_t = [[Tl(qT[i][:, :, b * 512:(b + 1) * 512]) for b in range(8)] for i in range(2)]
        kT_t = [[Tl(kT[i][:, b * 512:(b + 1) * 512]) for b in range(8)] for i in range(2)]
        vS_t = [[Tl(vS[i][:, 4 * b:4 * b + 4, :, :]) for b in range(8)] for i in range(2)]
        pT = [Tl(sb(nc, es, "pT%d" % i, [128, 512], BF16)[:, :]) for i in range(6)]
        oun = [Tl(sb(nc, es, "oun%d" % i, [64, 512], F32)[:, :]) for i in range(2)]
        rr = [Tl(sb(nc, es, "rr%d" % i, [65, 512], F32)[:, :]) for i in range(2)]
        oo = [Tl(sb(nc, es, "oo%d" % i, [64, 512], BF16)[:, :]) for i in range(2)]
        with ExitStack() as es1:
            xl = [Tl(sb(nc, es1, "xl%d%s" % (i, tag), [128, D], F32)[:, :]) for i in range(2)]
            xb = [Tl(sb(nc, es1, "xb%d%s" % (i, tag), [128, D], BF16)[:, :]) for i in range(2)]
            z = [Tl(sb(nc, es1, "z%d" % i, [128, 16], F32)[:, :]) for i in range(2)]
            za = [Tl(sb(nc, es1, "za%d" % i, [128, 16], F32)[:, :]) for i in range(2)]
            zm = [Tl(sb(nc, es1, "zm%d" % i, [128, 16], F32)[:, :]) for i in range(2)]
            tp = [Tl(ps(nc, es1, "tp%d%s" % (i, tag), [128, D], BF16)[:, :]) for i in range(2)]
            fl = [Tl(ps(nc, es1, "fl%d" % i, [128, 16], F32)[:, :]) for i in range(2)]
            cps = [Tl(ps(nc, es1, "cps%d" % i, [128, 16], F32)[:, :]) for i in range(2)]
            cqp = [Tl(ps(nc, es1, "cqp%d" % i, [16, 512], F32)[:, :]) for i in range(2)]
            for ti in range(NT):
                s = ti % 2
                load_x_transposed(P, C, Xin[ti], xl[s], xb[s], tp[s], xTa[:, :, ti * 128:(ti + 1) * 128], xTa_t[ti])
                for k in range(8):
                    P.mm(fl[s].ap, xTa[:, k, ti * 128:(ti + 1) * 128], wf.ap[:, k, :], k == 0, k == 7, [xTa_t[ti], wf], [fl[s]])
                P.v("dve", "tensor_tensor", [fl[s], bf_bc], [z[s]], z[s].ap, fl[s].ap, bf_bc.ap, ALU.add)
                if "abs" not in FOX_SKIP:
                    P.act(za[s].ap, z[s].ap, AF.Abs, [z[s]], [za[s]])
                else:
                    P.act(za[s].ap, z[s].ap, AF.Copy, [z[s]], [za[s]])
                if "exp" not in FOX_SKIP:
                    P.act(za[s].ap, za[s].ap, AF.Exp, [za[s]], [za[s]], scale=-1.0)
                if "ln" not in FOX_SKIP:
                    P.act(za[s].ap, za[s].ap, AF.Ln, [za[s]], [za[s]], bias=1.0)
                P.v("dve", "tensor_scalar_min", [z[s]], [zm[s]], zm[s].ap, z[s].ap, 0.0)
                P.v("dve", "tensor_sub", [zm[s], za[s]], [logf_t[ti]], logf_t[ti].ap, zm[s].ap, za[s].ap)
                c = cps[s]
                P.mm(c.ap, C.Lf.ap, logf_t[ti].ap, True, ti == 0, [C.Lf, logf_t[ti]], [c])
                if ti > 0:
                    P.mm(c.ap, C.E127f.ap, cum_t[ti - 1].ap, False, True, [C.E127f, cum_t[ti - 1]], [c])
                P.act(cum_t[ti].ap, c.ap, AF.Copy, [c], [cum_t[ti]])
                P.v("dve", "tensor_scalar_mul", [c], [ncum_t[ti]], ncum_t[ti].ap, c.ap, -1.0)
                P.v("dve", "tensor_copy", [c], [cumb_t[ti]], cumb_t[ti].ap, c.ap)
            for g in range(8):
                cq = cqp[g % 2]
                for j in range(4):
                    ti = 4 * g + j
                    P.mm(cq.ap[:, j * 128:(j + 1) * 128], cumb_t[ti].ap, C.E127b.ap, True, True, [cumb_t[ti], C.E127b], [cq])
                P.v("dve", "tensor_scalar_mul", [cq], [Cq_t[g]], Cq_t[g].ap, cq.ap, 1.0 / scale)
            P.barrier()
        sc = [Tl(ps(nc, es, "sc%d" % i, [128, 512], F32)[:, :]) for i in range(3)]
        oacc = [Tl(ps(nc, es, "oacc%d" % i, [65, 512], F32)[:, :]) for i in range(2)]
        bc = Tl(ps(nc, es, "bc", [128, 512], F32)[:, :])
        srow = [Tl(sb(nc, es, "srow%d" % i, [65, 512], F32)[:, :]) for i in range(2)]
        cqbc = [Tl(sb(nc, es, "cqbc%d" % i, [128, 512], F32)[:, :]) for i in range(2)]
        tmpS = [Tl(sb(nc, es, "tmpS%d" % i, [128, 512], F32)[:, :]) for i in range(5)]
        pj = [Tl(ps(nc, es, "pj%d" % i, [128, 512], F32)[:, :]) for i in range(2)]
        pjc = [0]
        for i in range(2):
            P.v("dve", "memset", [], vS_t[i], vS[i][:, :, :, 64:65], 1.0)
            P.v("pool", "memset", [], qT_t[i], qT[i][64:128, 0, :], 0.0)
            P.v("pool", "memset", [], qT_t[i], qT[i][0:64, 1, :], 0.0)

        def proj_items(g):
            sl = g % 2
            items = []

            def load_w():
                for j in range(3):
                    c0 = j * D + g * 128
                    P.dma("pool", wpair[sl][:, :, j * 128:(j + 1) * 128], w_v[:, :, c0:c0 + 128], [], [wpair_t[sl]])

            def qk_item(j, b):
                dst, dst_t = (qT, qT_t) if j == 0 else (kT, kT_t)

                def f():
                    acc = pj[pjc[0] % len(pj)]
                    pjc[0] += 1
                    for k in range(8):
                        P.mm(acc.ap, wpair[sl][:, k, j * 128:(j + 1) * 128], xTa[:, k, b * 512:(b + 1) * 512],
                             k == 0, k == 7, [wpair_t[sl]] + xTa_t[4 * b:4 * b + 4], [acc])
                    if j == 0:
                        P.act(qT[sl][0:64, 0, b * 512:(b + 1) * 512], acc.ap[0:64, :], AF.Copy, [acc], [qT_t[sl][b]])
                        P.act(qT[sl][64:128, 1, b * 512:(b + 1) * 512], acc.ap[64:128, :], AF.Copy, [acc], [qT_t[sl][b]])
                    else:
                        P.act(dst[sl][:, b * 512:(b + 1) * 512], acc.ap, AF.Copy, [acc], [dst_t[sl][b]])
                return f

            def v_item(b):
                def f():
                    acc = pj[pjc[0] % len(pj)]
                    pjc[0] += 1
                    for t in range(4):
                        ti = 4 * b + t
                        for k in range(8):
                            P.mm(acc.ap[:, t * 128:(t + 1) * 128], xTa[:, k, ti * 128:(ti + 1) * 128],
                                 wpair[sl][:, k, 256:384], k == 0, k == 7, [wpair_t[sl], xTa_t[ti]], [acc])
                    P.act(vS[sl][:, 4 * b:4 * b + 4, :, 0:64], acc.ap.rearrange("p (t h d) -> p t h d", t=4, h=2), AF.Copy,
                          [acc], [vS_t[sl][b]])
                return f

            items.append(load_w)
            for b in range(8):
                items.append(qk_item(0, b))
                items.append(qk_item(1, b))
                items.append(v_item(b))
            return items

        def stageA(i, t):
            g, hh, b, kt = t["u"], t["hh"], t["b"], t["kt"]
            sl = g % 2
            h = 2 * g + hh
            pb = 64 * hh
            di, c0 = diag_range(kt, b)
            qs = slice(b * 512 + c0, (b + 1) * 512)
            s_ = sc[i % len(sc)]
            p_ = pT[i % len(pT)]
            cb_ = cqbc[t["pc"] % 2]
            if kt == 0:
                P.mm(bc.ap, C.sel.ap[:, h, :], Cq[:, b * 512:(b + 1) * 512], True, True, [C.sel, Cq_t[b]], [bc])
                P.act(cb_.ap, bc.ap, AF.Copy, [bc], [cb_])
            tm = tmpS[i % len(tmpS)]
            P.mm(s_.ap[:, c0:512], kT[sl][:, kt * 128:(kt + 1) * 128], qT[sl][:, hh, qs], True, True,
                 [kT_t[sl][kt // 4], qT_t[sl][b]], [s_])
            P.v("dve", "tensor_tensor", [s_, cb_], [tm], tm.ap[:, c0:512], s_.ap[:, c0:512], cb_.ap[:, c0:512], ALU.add)
            P.act(p_.ap[:, c0:512], tm.ap[:, c0:512], AF.Exp, [tm, ncum_t[kt]], [p_], bias=ncum[:, kt, h:h + 1], scale=scale)
            if di >= 0:
                P.v("pool", "tensor_tensor", [p_, C.tri], [p_], p_.ap[:, c0:c0 + 128], p_.ap[:, c0:c0 + 128], C.tri.ap, ALU.mult)

        def stageB(i, t):
            g, hh, b, kt, pc = t["u"], t["hh"], t["b"], t["kt"], t["pc"]
            sl = g % 2
            di, c0 = diag_range(kt, b)
            p_ = pT[i % len(pT)]
            oa = oacc[pc % len(oacc)]
            P.mm(oa.ap[:, c0:512], vS[sl][:, kt, hh, :], p_.ap[:, c0:512], kt == 0, kt == 4 * b + 3,
                 [vS_t[sl][kt // 4], p_], [oa])

        def epilogue(t):
            g, hh, b, pc = t["u"], t["hh"], t["b"], t["pc"]
            h = 2 * g + hh
            oa = oacc[pc % len(oacc)]
            u = oun[pc % 2]
            r_ = rr[pc % 2]
            o_ = oo[pc % 2]
            P.v("dve", "tensor_copy", [oa], [u], u.ap, oa.ap[0:64, :])
            sr = srow[pc % 2]
            P.act(sr.ap[64:65, :], oa.ap[64:65, :], AF.Ln, [oa], [sr])
            P.act(r_.ap[64:65, :], sr.ap[64:65, :], AF.Exp, [sr], [r_], scale=-1.0)

            def late():
                P.mm(bc.ap[0:64, :], C.onesf.ap[64:65, 0:64], r_.ap[64:65, :], True, True, [C.onesf, r_], [bc])
                P.v("dve", "tensor_tensor", [u, bc], [o_], o_.ap, u.ap, bc.ap[0:64, :], ALU.mult)
                P.dma("sp", OT[h, :, b * 512:(b + 1) * 512], o_.ap, [o_], [OT_t[h][b]])
            return late

        tiles = []
        pc = 0
        for g in range(FOX_PAIRS):
            for hh in range(2):
                for b in range(8):
                    for kt in range(4 * b + 4):
                        tiles.append(dict(u=g, hh=hh, b=b, kt=kt, pc=pc, last=(kt == 4 * b + 3), bi=hh * 8 + b))
                    pc += 1
        drive_attention(tiles, ATT_LA, stageA, stageB, epilogue, proj_items, FOX_PAIRS, 16)
        P.barrier()


def phase_mla(P, C, CQT, CQT_t, CKVT, CKVT_t, KRT, KRT_t, wuq_d, wup_d, OT, OT_t, tag="ml"):
    nc = P.nc
    scale = (NOPE + ROPE) ** -0.5
    uq_v = wuq_d.rearrange("(c p) h e -> p c h e", p=128)
    up_v = wup_d.rearrange("(c p) h e -> p c h e", p=128)
    with ExitStack() as es:
        phase_consts(P, C, es, ["tri", "onesf", "cosT", "sinT"], tag)
        cqs = sb(nc, es, "cqA", [128, 3, S], BF16)
        ckvs = sb(nc, es, "ckvA", [128, 2, S], BF16)
        krts = sb(nc, es, "krA", [128, S], BF16)
        cq_t, ckv_t, krt_t = [], [], []
        for b in range(8):
            cs = slice(b * 512, (b + 1) * 512)
            cq_t.append(Tl(cqs[:, :, cs]))
            ckv_t.append(Tl(ckvs[:, :, cs]))
            krt_t.append(Tl(krts[:, cs]))
            P.dma("sp", ckv_t[b].ap, CKVT[:, :, cs].rearrange("c p t -> p c t"), [CKVT_t[b]], [ckv_t[b]])
            P.dma("sp", cq_t[b].ap, CQT[:, :, cs].rearrange("c p t -> p c t"), [CQT_t[b]], [cq_t[b]])
            P.v("pool", "memset", [], [krt_t[b]], krts[64:128, cs], 0.0)
            P.dma("sp", krts[0:64, cs], KRT[:, cs], [KRT_t[b]], [krt_t[b]])
        wq = [Tl(sb(nc, es, "wq%d" % i, [128, 3, 192], BF16)[:, :, :]) for i in range(2)]
        wqs = [Tl(sb(nc, es, "wqs%d" % i, [128, 3, 64], BF16)[:, :, :]) for i in range(2)]
        wkv = [Tl(sb(nc, es, "wkv%d" % i, [128, 2, 256], BF16)[:, :, :]) for i in range(2)]
        kTn = [sb(nc, es, "kTn%d" % i, [128, S], BF16) for i in range(2)]
        vH = [sb(nc, es, "vH%d" % i, [128, NT, 128], BF16) for i in range(2)]
        qn = [sb(nc, es, "qn%d" % i, [128, S], BF16) for i in range(2)]
        qr = [sb(nc, es, "qr%d" % i, [128, S], BF16) for i in range(2)]
        kTn_t = [[Tl(kTn[i][:, b * 512:(b + 1) * 512]) for b in range(8)] for i in range(2)]
        vH_t = [[Tl(vH[i][:, 4 * b:4 * b + 4, :]) for b in range(8)] for i in range(2)]
        qn_t = [[Tl(qn[i][:, b * 512:(b + 1) * 512]) for b in range(8)] for i in range(2)]
        qr_t = [[Tl(qr[i][:, b * 512:(b + 1) * 512]) for b in range(8)] for i in range(2)]
        for i in range(2):
            P.v("pool", "memset", [], qr_t[i], qr[i][64:128, :], 0.0)
        pT = [Tl(sb(nc, es, "pT%d" % i, [128, 512], BF16)[:, :]) for i in range(6)]
        t1 = Tl(sb(nc, es, "t1" + tag, [64, 512], F32)[:, :])
        t2 = Tl(sb(nc, es, "t2" + tag, [64, 512], F32)[:, :])
        rr = [Tl(sb(nc, es, "rr%d" % i, [128, 512], F32)[:, :]) for i in range(2)]
        oo = [Tl(sb(nc, es, "oo%d" % i, [128, 512], BF16)[:, :]) for i in range(2)]
        sc = [Tl(ps(nc, es, "sc%d" % i, [128, 512], F32)[:, :]) for i in range(3)]
        oacc = [Tl(ps(nc, es, "oacc%d" % i, [128, 512], F32)[:, :]) for i in range(2)]
        sacc = [Tl(sb(nc, es, "sacc%d" % i, [128, 512], F32)[:, :]) for i in range(2)]
        saccP = [Tl(sb(nc, es, "saccP%d" % i, [128, 512], F32)[:, :]) for i in range(2)]
        sm = Tl(ps(nc, es, "sm", [128, 512], F32)[:, :])
        pj = [Tl(ps(nc, es, "pj%d" % i, [128, 512], F32)[:, :]) for i in range(2)]
        pjc = [0]

        def prep_items(h):
            sl = h % 2
            items = []

            def load_w():
                P.dma("pool", wq[sl].ap, uq_v[:, :, h, :], [], [wq[sl]])
                P.dma("pool", wqs[sl].ap[:, :, 0:32], uq_v[:, :, h, NOPE + 32:NOPE + 64], [], [wqs[sl]])
                P.dma("pool", wqs[sl].ap[:, :, 32:64], uq_v[:, :, h, NOPE:NOPE + 32], [], [wqs[sl]])
                P.dma("pool", wkv[sl].ap, up_v[:, :, h, :], [], [wkv[sl]])

            def nxt():
                a = pj[pjc[0] % len(pj)]
                pjc[0] += 1
                return a

            def k_item(b):
                def f():
                    acc = nxt()
                    for c in range(2):
                        P.mm(acc.ap, wkv[sl].ap[:, c, 0:128], ckvs[:, c, b * 512:(b + 1) * 512], c == 0, c == 1, [wkv[sl], ckv_t[b]], [acc])
                    P.act(kTn_t[sl][b].ap, acc.ap, AF.Copy, [acc], [kTn_t[sl][b]])
                return f

            def v_item(b):
                def f():
                    acc = nxt()
                    for t in range(4):
                        ti = 4 * b + t
                        for c in range(2):
                            P.mm(acc.ap[:, t * 128:(t + 1) * 128], ckvs[:, c, ti * 128:(ti + 1) * 128], wkv[sl].ap[:, c, 128:256],
                                 c == 0, c == 1, [wkv[sl], ckv_t[b]], [acc])
                    P.act(vH_t[sl][b].ap, acc.ap.rearrange("p (t d) -> p t d", t=4), AF.Copy, [acc], [vH_t[sl][b]])
                return f

            def qn_item(b):
                def f():
                    acc = nxt()
                    for c in range(3):
                        P.mm(acc.ap, wq[sl].ap[:, c, 0:128], cqs[:, c, b * 512:(b + 1) * 512], c == 0, c == 2, [wq[sl], cq_t[b]], [acc])
                    P.act(qn_t[sl][b].ap, acc.ap, AF.Copy, [acc], [qn_t[sl][b]])
                return f

            def qr_item(b):
                def f():
                    cs = slice(b * 512, (b + 1) * 512)
                    a1 = nxt()
                    for c in range(3):
                        P.mm(a1.ap[0:64, :], wq[sl].ap[:, c, 128:192], cqs[:, c, cs], c == 0, c == 2, [wq[sl], cq_t[b]], [a1])
                    P.v("dve", "tensor_tensor", [a1, C.cosT], [t1], t1.ap, a1.ap[0:64, :], C.cosT.ap[:, cs], ALU.mult)
                    a2 = nxt()
                    for c in range(3):
                        P.mm(a2.ap[0:64, :], wqs[sl].ap[:, c, :], cqs[:, c, cs], c == 0, c == 2, [wqs[sl], cq_t[b]], [a2])
                    P.v("dve", "tensor_tensor", [a2, C.sinT], [t2], t2.ap, a2.ap[0:64, :], C.sinT.ap[:, cs], ALU.mult)
                    P.v("dve", "tensor_tensor", [t1, t2], [qr_t[sl][b]], qr[sl][0:64, b * 512:(b + 1) * 512], t1.ap, t2.ap, ALU.add)
                return f

            items.append(load_w)
            for b in range(8):
                items += [k_item(b), v_item(b), qn_item(b), qr_item(b)]
            return items

        def stageA(i, t):
            h, b, kt = t["u"], t["b"], t["kt"]
            sl = h % 2
            di, c0 = diag_range(kt, b)
            qs = slice(b * 512 + c0, (b + 1) * 512)
            ks = slice(kt * 128, (kt + 1) * 128)
            s_ = sc[i % len(sc)]
            p_ = pT[i % len(pT)]
            P.mm(s_.ap[:, c0:512], kTn[sl][:, ks], qn[sl][:, qs], True, False, [kTn_t[sl][kt // 4], qn_t[sl][b]], [s_])
            P.mm(s_.ap[:, c0:512], krts[:, ks], qr[sl][:, qs], False, True, [krt_t[kt // 4], qr_t[sl][b]], [s_])
            P.act(p_.ap[:, c0:512], s_.ap[:, c0:512], AF.Exp, [s_], [p_], scale=scale)
            if di >= 0:
                P.v("pool", "tensor_tensor", [p_, C.tri], [p_], p_.ap[:, c0:c0 + 128], p_.ap[:, c0:c0 + 128], C.tri.ap, ALU.mult)

        def stageB(i, t):
            h, b, kt, pc = t["u"], t["b"], t["kt"], t["pc"]
            sl = h % 2
            di, c0 = diag_range(kt, b)
            p_ = pT[i % len(pT)]
            oa = oacc[pc % len(oacc)]
            sa = sacc[pc % 2]
            nk = 4 * b + 4
            P.mm(oa.ap[:, c0:512], vH[sl][:, kt, :], p_.ap[:, c0:512], kt == 0, kt == nk - 1, [vH_t[sl][kt // 4], p_], [oa])
            sp_ = saccP[pc % 2]
            if kt == 0:
                P.v("dve", "tensor_copy", [p_], [sa], sa.ap, p_.ap)
                P.v("pool", "memset", [], [sp_], sp_.ap, 0.0)
            elif kt % 3 == 1:
                P.v("pool", "tensor_tensor", [sp_, p_], [sp_], sp_.ap[:, c0:512], sp_.ap[:, c0:512], p_.ap[:, c0:512], ALU.add)
            else:
                P.v("dve", "tensor_tensor", [sa, p_], [sa], sa.ap[:, c0:512], sa.ap[:, c0:512], p_.ap[:, c0:512], ALU.add)

        def epilogue(t):
            h, b, pc = t["u"], t["b"], t["pc"]
            oa = oacc[pc % len(oacc)]
            sa = sacc[pc % 2]
            r_ = rr[pc % 2]
            o_ = oo[pc % 2]
            sp_ = saccP[pc % 2]

            def late():
                P.mm(sm.ap, C.onesf.ap, sa.ap, True, False, [C.onesf, sa], [sm])
                P.mm(sm.ap, C.onesf.ap, sp_.ap, False, True, [C.onesf, sp_], [sm])
                P.v("dve", "reciprocal", [sm], [r_], r_.ap, sm.ap)
                P.v("dve", "tensor_tensor", [oa, r_], [o_], o_.ap, oa.ap, r_.ap, ALU.mult)
                P.dma("sp", OT[h, :, b * 512:(b + 1) * 512], o_.ap, [o_], [OT_t[h][b]])
            return late

        tiles = []
        pc = 0
        for h in range(MLA_H):
            for b in range(8):
                for kt in range(4 * b + 4):
                    tiles.append(dict(u=h, b=b, kt=kt, pc=pc, last=(kt == 4 * b + 3), bi=b))
                pc += 1
        drive_attention(tiles, ATT_LA, stageA, stageB, epilogue, prep_items, MLA_H, 8, defer=MLA_DEFER)
        P.barrier()


def build_program(stages=99, debug=False, first=0, part=None):
    nc = bass.Bass("TRN2", target_bir_lowering=False)
    dbg_kind = "ExternalOutput" if debug else "Internal"
    if part == 0:
        first, stages = 0, 6
    elif part == 1:
        first, stages = 6, 12

    def inp(name, shape, dt=F32):
        return nc.dram_tensor(name, list(shape), dt, kind="ExternalInput").ap()

    x = inp("x", [S, D])
    p = inp("p", [DEPTH, S, PLE])
    ffn_w_in = [inp("ffn1_w_in", [DEPTH, D, 2 * DFF]), inp("ffn2_w_in", [DEPTH, D, 2 * DFF])]
    ffn_w_out = [inp("ffn1_w_out", [DEPTH, DFF, D]), inp("ffn2_w_out", [DEPTH, DFF, D])]
    ln_g = inp("ln_g", [DEPTH * 4, D])
    ln_b = inp("ln_b", [DEPTH * 4, D])
    ple_w_gate = inp("ple_w_gate", [DEPTH, D, D])
    ple_b_gate = inp("ple_b_gate", [DEPTH, D])
    ple_w_proj = inp("ple_w_proj", [DEPTH, PLE, D])
    fox_w_in = inp("fox_w_in", [D, 3 * D + FOX_H])
    fox_b_f = inp("fox_b_f", [1, FOX_H])
    fox_w_o = inp("fox_w_o", [D, D])
    mla_w_dq = inp("mla_w_dq", [D, QL])
    mla_q_norm = inp("mla_q_norm", [1, QL])
    mla_w_uq = inp("mla_w_uq", [QL, MLA_H, NOPE + ROPE])
    mla_w_o = inp("mla_w_o", [MLA_H * VD, D])
    kv_w_down = inp("kv_w_down", [D, KVL + ROPE])
    kv_norm = inp("kv_norm", [1, KVL])
    kv_w_up = inp("kv_w_up", [KVL, MLA_H, NOPE + VD])
    c_ident = inp("c_ident", [128, 128], BF16)
    c_tri = inp("c_tri", [128, 128], BF16)
    c_Lf = inp("c_Lf", [128, 128])
    c_E127f = inp("c_E127f", [128, 128])
    c_E127b = inp("c_E127b", [128, 128], BF16)
    c_sel = inp("c_sel", [16, 16, 128], BF16)
    c_onesf = inp("c_onesf", [128, 128])
    c_onesb = inp("c_onesb", [128, 128], BF16)
    c_cosT = inp("c_cosT", [64, S])
    c_sinT = inp("c_sinT", [64, S])

    hand = {None: dbg_kind, 0: "ExternalOutput", 1: "ExternalInput"}[part]
    out = nc.dram_tensor("out", [S, D], F32, kind="ExternalOutput" if part != 0 else "Internal").ap()
    XA = nc.dram_tensor("XA", [S, D], F32, kind=dbg_kind).ap()
    XB = nc.dram_tensor("XB", [S, D], F32, kind=hand).ap()
    OT0 = nc.dram_tensor("OT0", [16, 64, S], BF16, kind=dbg_kind).ap()
    OT1 = nc.dram_tensor("OT1", [16, 128, S], BF16, kind=dbg_kind).ap()
    CKVT = nc.dram_tensor("CKVT", [2, 128, S], BF16, kind=hand).ap()
    KRT = nc.dram_tensor("KRT", [64, S], BF16, kind=hand).ap()
    CQT = nc.dram_tensor("CQT", [3, 128, S], BF16, kind=dbg_kind).ap()

    def tiles(ap):
        return [Tl(ap[i * 128:(i + 1) * 128, :]) for i in range(NT)]

    Xx, XAt, XBt, Outt = tiles(x), tiles(XA), tiles(XB), tiles(out)
    Pt = [[Tl(p[l, i * 128:(i + 1) * 128, :]) for i in range(NT)] for l in range(DEPTH)]
    OT0_t = [[Tl() for b in range(8)] for h in range(16)]
    OT1_t = [[Tl() for b in range(8)] for h in range(16)]
    CKVT_t = [Tl() for b in range(8)]
    KRT_t = [Tl() for b in range(8)]
    CQT_t = [Tl() for b in range(8)]

    with ExitStack() as es:
        P = Prog(nc, es)
        C = Ctx()

        C.ident = load_const(P, es, "k_ident", c_ident, [128, 128], BF16)
        C.d = dict(tri=(c_tri, [128, 128], BF16), Lf=(c_Lf, [128, 128], F32), E127f=(c_E127f, [128, 128], F32),
                   E127b=(c_E127b, [128, 128], BF16), sel=(c_sel, [16, 16, 128], BF16), onesf=(c_onesf, [128, 128], F32),
                   onesb=(c_onesb, [128, 128], BF16), cosT=(c_cosT, [64, S], F32), sinT=(c_sinT, [64, S], F32))
        C.eps_ln = Tl(sb(nc, es, "eps_ln", [128, 1], F32)[:, :])
        P.v("dve", "memset", [], [C.eps_ln], C.eps_ln.ap, LN_EPS / (ALPHA * ALPHA))

        def lng(l, i):
            return ln_g[4 * l + i:4 * l + i + 1, :], ln_b[4 * l + i:4 * l + i + 1, :]

        seq = []
        seq.append(lambda last: phase_ffn(P, C, Xx, Outt if last else XAt, ffn_w_in[0][0], ffn_w_out[0][0], *lng(0, 0), "a"))
        seq.append(lambda last: phase_fox(P, C, XAt, fox_w_in, fox_b_f, OT0, OT0_t))
        seq.append(lambda last: phase_attout(P, C, XAt, Outt if last else XBt, OT0.rearrange("(g e) d t -> g (e d) t", e=2),
                                             lambda c, b: [OT0_t[2 * c][b], OT0_t[2 * c + 1][b]], 8, fox_w_o, *lng(0, 1), "b"))
        seq.append(lambda last: phase_ffn(P, C, XBt, Outt if last else XAt, ffn_w_in[1][0], ffn_w_out[1][0], *lng(0, 2), "c", pre=PRE[0]))
        seq.append(lambda last: phase_ple(P, C, XAt, Outt if last else XBt, Pt[0], ple_w_gate[0], ple_b_gate[0:1, :], ple_w_proj[0],
                                          *lng(0, 3), "d"))
        seq.append(lambda last: phase_latent(P, C, XBt, kv_w_down, KVL, kv_norm, CKVT, CKVT_t, "e",
                                             rope_w_d=kv_w_down, KRT=KRT, KRT_t=KRT_t))
        seq.append(lambda last: phase_ffn(P, C, XBt, Outt if last else XAt, ffn_w_in[0][1], ffn_w_out[0][1], *lng(1, 0), "f", pre=PRE[0]))
        seq.append(lambda last: phase_latent(P, C, XAt, mla_w_dq, QL, mla_q_norm, CQT, CQT_t, "g"))
        seq.append(lambda last: phase_mla(P, C, CQT, CQT_t, CKVT, CKVT_t, KRT, KRT_t, mla_w_uq, kv_w_up, OT1, OT1_t))
        seq.append(lambda last: phase_attout(P, C, XAt, Outt if last else XBt, OT1, lambda c, b: [OT1_t[c][b]], 16, mla_w_o,
                                             *lng(1, 1), "h"))
        seq.append(lambda last: phase_ffn(P, C, XBt, Outt if last else XAt, ffn_w_in[1][1], ffn_w_out[1][1], *lng(1, 2), "i", pre=PRE[0]))
        seq.append(lambda last: phase_ple(P, C, XAt, Outt, Pt[1], ple_w_gate[1], ple_b_gate[1:2, :], ple_w_proj[1], *lng(1, 3), "j"))
        n = min(stages, len(seq))
        ffn_w = {3: (ffn_w_in[1][0], "c"), 6: (ffn_w_in[0][1], "f"), 10: (ffn_w_in[1][1], "i")}
        i = first
        while i < n:
            if i + 1 in ffn_w and i + 1 < n and PREFETCH:
                with ExitStack() as pf:
                    PRE[0] = ffn_prefetch(P, pf, *ffn_w[i + 1])
                    seq[i](False)
                    seq[i + 1](i + 1 == n - 1 and part != 0)
                    PRE[0] = None
                i += 2
            else:
                seq[i](i == n - 1 and part != 0)
                i += 1
        P.barrier()
    return nc


def host_consts():
    import ml_dtypes
    bf = ml_dtypes.bfloat16
    idx = np.arange(128)
    tri = (idx[:, None] <= idx[None, :]).astype(np.float32)
    e127 = np.zeros((128, 128), np.float32)
    e127[127, :] = 1.0
    sel = np.zeros((16, 16, 128), np.float32)
    for h in range(16):
        sel[h, h, :] = 1.0
    half = ROPE // 2
    inv = (np.float32(10000.0) ** (-np.arange(half, dtype=np.float32) * np.float32(2.0 / ROPE))).astype(np.float32)
    ang = (np.arange(S, dtype=np.float32)[:, None] * inv[None, :]).astype(np.float32)
    cos = np.cos(ang).astype(np.float32).T
    sin = np.sin(ang).astype(np.float32).T
    return {
        "c_ident": np.eye(128, dtype=np.float32).astype(bf),
        "c_tri": tri.astype(bf),
        "c_Lf": tri,
        "c_E127f": e127,
        "c_E127b": e127.astype(bf),
        "c_sel": sel.astype(bf),
        "c_onesf": np.ones((128, 128), np.float32),
        "c_onesb": np.ones((128, 128), np.float32).astype(bf),
        "c_cosT": np.ascontiguousarray(np.concatenate([cos, cos], 0)),
        "c_sinT": np.ascontiguousarray(np.concatenate([-sin, sin], 0)),
    }


def make_in_maps(inputs, cores=range(N_CORES)):
    consts = host_consts()
    f = lambda a: np.ascontiguousarray(np.asarray(a, dtype=np.float32))
    shared = {
        "ln_g": f(inputs["ln_g"]).reshape(DEPTH * 4, D), "ln_b": f(inputs["ln_b"]).reshape(DEPTH * 4, D),
        "fox_w_in": f(inputs["fox_w_in"][0]), "fox_b_f": f(inputs["fox_b_f"]).reshape(1, FOX_H), "fox_w_o": f(inputs["fox_w_o"][0]),
        "mla_w_dq": f(inputs["mla_w_dq"][0]), "mla_q_norm": f(inputs["mla_q_norm"]).reshape(1, QL),
        "mla_w_uq": f(inputs["mla_w_uq"][0]), "mla_w_o": f(inputs["mla_w_o"][0]),
        "kv_w_down": f(inputs["kv_w_down"]), "kv_norm": f(inputs["kv_norm"]).reshape(1, KVL), "kv_w_up": f(inputs["kv_w_up"]),
    }
    for k in ("ffn1_w_in", "ffn1_w_out", "ffn2_w_in", "ffn2_w_out", "ple_w_gate", "ple_b_gate", "ple_w_proj"):
        shared[k] = f(inputs[k])
    maps = []
    for c in cores:
        m = dict(consts)
        m.update(shared)
        m["x"] = f(inputs["x"][c])
        m["p"] = f(inputs["p"][:, c])
        maps.append(m)
    return maps


FUSED = True


def kernel(**inputs):
    inputs = {k: np.asarray(v) for k, v in inputs.items()}
    maps = make_in_maps(inputs)
    cores = list(range(N_CORES))
    if FUSED:
        res = run_bass_kernel_spmd(build_program(), maps, core_ids=cores)
    else:
        ra = run_bass_kernel_spmd(build_program(part=0), maps, core_ids=cores)
        for m, r in zip(maps, ra.results):
            for k in ("XB", "CKVT", "KRT"):
                m[k] = np.asarray(r[k])
        res = run_bass_kernel_spmd(build_program(part=1), maps, core_ids=cores)
    return np.stack([np.asarray(r["out"]) for r in res.results], axis=0).astype(np.float32)
```

```python
import numpy as np
from contextlib import ExitStack

import concourse.bass as bass
import concourse.mybir as mybir
from concourse.bass_utils import run_bass_kernel_spmd

F32 = mybir.dt.float32
BF16 = mybir.dt.bfloat16
AF = mybir.ActivationFunctionType
ALU = mybir.AluOpType

N_CORES = 8
S = 4096
D = 1024
NT = S // 128
DFF = 2816
NFC = DFF // 128
DEPTH = 2
ALPHA = (2 * DEPTH) ** 0.25
LN_EPS = 1e-5
RMS_EPS = 1e-6
PLE = 256
FOX_H = 16
FOX_D = 64
MLA_H = 16
NOPE = 128
ROPE = 64
VD = 128
QL = 384
KVL = 256


class Tl:
    __slots__ = ("ap", "w", "r", "rd")

    def __init__(self, ap=None):
        self.ap = ap
        self.w = None
        self.r = {}
        self.rd = []


class Op:
    __slots__ = ("eng", "fn", "deps", "signal", "ev", "is_dma", "barrier", "noattach")

    def __init__(self, eng, fn, is_dma):
        self.eng = eng
        self.fn = fn
        self.deps = set()
        self.signal = False
        self.ev = None
        self.is_dma = is_dma
        self.barrier = False
        self.noattach = False


ATTACH_WAITS = False
PREFETCH = True
PRE = [None]
FOX_PAIRS = 8
MLA_DEFER = 3
ATT_LA = 4
FOX_SKIP = ()


class Prog:
    ENGS = ("pe", "act", "dve", "pool", "sp")

    def __init__(self, nc, es):
        self.nc = nc
        self.es = es
        self.nsem = 0
        self.ops = []
        self.emitted = 0
        self.eobj = {"pe": nc.tensor, "act": nc.scalar, "dve": nc.vector, "pool": nc.gpsimd, "sp": nc.sync}
        self.sem = {}
        for e in ("pe", "act", "dve", "pool"):
            self.sem[e] = es.enter_context(nc.semaphore("s_" + e))
        self.cnt = {e: 0 for e in self.sem}
        self.dma_pool = {"sp": [], "pool": [], "act": []}
        for q, n in (("sp", 32), ("pool", 16), ("act", 0)):
            for i in range(n):
                self.dma_pool[q].append([es.enter_context(nc.semaphore("d_%s%d" % (q, i))), 0])
        self.dma_rr = {"sp": 0, "pool": 0, "act": 0}
        self.waited = {e: {} for e in self.ENGS}
        self.last_op = {e: None for e in self.ENGS}
        self.dma_since_barrier = []

    def add(self, eng, fn, reads=(), writes=(), is_dma=False):
        i = len(self.ops)
        op = Op(eng, fn, is_dma)
        ops = self.ops
        deps = op.deps
        for t in reads:
            if t.w is not None:
                deps.add(t.w)
        for t in writes:
            if t.w is not None:
                j = t.w
                if is_dma or ops[j].is_dma or ops[j].eng != eng:
                    deps.add(j)
            for e, j in t.r.items():
                if is_dma or e != eng:
                    deps.add(j)
            for j in t.rd:
                deps.add(j)
        for j in deps:
            ops[j].signal = True
        for t in reads:
            if is_dma:
                t.rd.append(i)
            else:
                t.r[eng] = i
        for t in writes:
            t.w = i
            t.r = {}
            t.rd = []
        if is_dma:
            op.signal = True
            self.dma_since_barrier.append(i)
        self.last_op[eng] = i
        ops.append(op)
        return i

    def barrier(self):
        lasts = [j for j in self.last_op.values() if j is not None]
        for j in lasts:
            self.ops[j].signal = True
        for e in self.ENGS:
            op = Op(e, None, False)
            op.barrier = True
            op.deps = set(lasts) | set(self.dma_since_barrier)
            self.ops.append(op)
        self.dma_since_barrier = []
        self.flush()
        for e in self.sem:
            if self.cnt[e] > 1500:
                self.nsem += 1
                self.sem[e] = self.es.enter_context(self.nc.semaphore("s_%s_%d" % (e, self.nsem)))
                self.cnt[e] = 0

    def _wait(self, eng, sem, val):
        w = self.waited[eng]
        key = sem.num
        if w.get(key, 0) < val:
            self.eobj[eng].wait_ge(sem, val)
            w[key] = val

    def flush(self):
        ops = self.ops
        for i in range(self.emitted, len(ops)):
            op = ops[i]
            need = {}
            for j in op.deps:
                ev = ops[j].ev
                if ev is None:
                    continue
                sem, val = ev
                k = sem.num
                if k not in need or need[k][1] < val:
                    need[k] = (sem, val)
            w = self.waited[op.eng]
            pend = [need[k] for k in sorted(need) if w.get(k, 0) < need[k][1]]
            attach = None
            if pend and ATTACH_WAITS and not op.barrier and not op.is_dma and not op.noattach:
                attach = pend.pop()
            for sem, val in pend:
                self._wait(op.eng, sem, val)
            if op.barrier:
                continue
            if op.is_dma:
                pool = self.dma_pool[op.eng]
                slot = pool[self.dma_rr[op.eng] % len(pool)]
                self.dma_rr[op.eng] += 1
                sw = None
                if slot[1] > 0 and w.get(slot[0].num, 0) < slot[1]:
                    if ATTACH_WAITS:
                        sw = (slot[0], slot[1])
                    else:
                        self._wait(op.eng, slot[0], slot[1])
                ins = op.fn()
                if sw is not None:
                    ins._wait_ge(sw[0], sw[1])
                    w[sw[0].num] = sw[1]
                slot[1] += 16
                ins.then_inc(slot[0], 16)
                op.ev = (slot[0], slot[1])
            else:
                ins = op.fn()
                if attach is not None:
                    ins._wait_ge(attach[0], attach[1])
                    w[attach[0].num] = attach[1]
                if op.signal:
                    self.cnt[op.eng] += 1
                    ins.then_inc(self.sem[op.eng], 1)
                    op.ev = (self.sem[op.eng], self.cnt[op.eng])
            op.fn = None
        self.emitted = len(ops)

    def dma(self, q, out, in_, reads, writes):
        e = self.eobj[q]
        return self.add(q, lambda: e.dma_start(out=out, in_=in_), reads, writes, is_dma=True)

    def mm(self, out, lhsT, rhs, start, stop, reads, writes):
        t = self.nc.tensor
        i = self.add("pe", lambda: t.matmul(out, lhsT, rhs, start=start, stop=stop), reads, writes)
        if lhsT.dtype == F32:
            self.ops[i].noattach = True
        return i

    def tr(self, out, in_, ident, reads, writes):
        t = self.nc.tensor
        return self.add("pe", lambda: t.transpose(out, in_, ident), reads, writes)

    def act(self, out, in_, func, reads, writes, bias=None, scale=None, accum_out=None):
        s = self.nc.scalar
        kw = {}
        if bias is not None:
            kw["bias"] = bias
        if scale is not None:
            kw["scale"] = scale
        if accum_out is not None:
            kw["accum_out"] = accum_out
        return self.add("act", lambda: s.activation(out, in_, func, **kw), reads, writes)

    def v(self, eng, name, reads, writes, *a, **kw):
        e = self.eobj[eng]
        f = getattr(e, name)
        return self.add(eng, lambda: f(*a, **kw), reads, writes)


class Ctx:
    pass


def load_x_transposed(P, C, Xin_t, xl, xb, tp, xT_ap, xT_t):
    nc = P.nc
    P.dma("sp", xl.ap, Xin_t.ap, [Xin_t], [xl])
    P.act(xb.ap, xl.ap, AF.Copy, [xl], [xb])
    for k in range(8):
        P.tr(tp.ap[:, k * 128:(k + 1) * 128], xb.ap[:, k * 128:(k + 1) * 128], C.ident.ap, [xb, C.ident], [tp])
    P.v("dve", "tensor_copy", [tp], [xT_t], xT_ap, tp.ap.rearrange("p (k t) -> p k t", k=8))


def post_norm_tile(P, C, xr, g_bc, b_bc, Xout_t, st, mv, rs, nm):
    eps = LN_EPS / (ALPHA * ALPHA)
    P.v("dve", "bn_stats", [xr], [st], st.ap[:, 0:6], xr.ap[:, 0:512])
    P.v("dve", "bn_stats", [xr], [st], st.ap[:, 6:12], xr.ap[:, 512:1024])
    P.v("dve", "bn_aggr", [st], [mv], mv.ap, st.ap)
    P.act(rs.ap, mv.ap[:, 1:2], AF.Sqrt, [mv], [rs], bias=C.eps_ln.ap, scale=1.0)
    P.v("dve", "reciprocal", [rs], [rs], rs.ap, rs.ap)
    P.v("dve", "scalar_tensor_tensor", [mv, rs], [nm], nm.ap, mv.ap[:, 0:1], -1.0, rs.ap, ALU.mult, ALU.mult)
    P.act(xr.ap, xr.ap, AF.Identity, [xr, rs, nm], [xr], bias=nm.ap, scale=rs.ap)
    P.v("pool", "tensor_tensor", [xr, g_bc], [xr], xr.ap, xr.ap, g_bc.ap, ALU.mult)
    P.v("pool", "tensor_tensor", [xr, b_bc], [xr], xr.ap, xr.ap, b_bc.ap, ALU.add)
    P.dma("pool", Xout_t.ap, xr.ap, [xr], [Xout_t])


def load_bcast(P, es, name, dram_row_ap, n):
    nc = P.nc
    t = Tl(sb(nc, es, name, [128, n], F32)[:, :])
    P.dma("sp", t.ap, dram_row_ap.partition_broadcast(128), [], [t])
    return t


_uid = [0]


def sb(nc, es, name, shape, dt):
    _uid[0] += 1
    return es.enter_context(nc.sbuf_tensor("%s_%d" % (name, _uid[0]), shape, dt))


def ps(nc, es, name, shape, dt):
    _uid[0] += 1
    return es.enter_context(nc.psum_tensor("%s_%d" % (name, _uid[0]), shape, dt))


NPRE = 5


def ffn_prefetch(P, es, w_in_d, tag):
    nc = P.nc
    wpre = sb(nc, es, "wpre" + tag, [128, 8, 2 * NPRE * 256], BF16)
    w_in_v = w_in_d.rearrange("(k p) n -> p k n", p=128)
    tl = {}
    for gi in range(NPRE):
        for half in range(2):
            c0 = half * DFF + gi * 256
            o0 = (half * NPRE + gi) * 256
            t = Tl(wpre[:, :, o0:o0 + 256])
            P.dma("pool", t.ap, w_in_v[:, :, c0:c0 + 256], [], [t])
            tl[(half, gi)] = (t, o0)
    return wpre, tl


def phase_ffn(P, C, Xin, Xout, w_in_d, w_out_d, g_d, b_d, tag, pre=None):
    nc = P.nc
    with ExitStack() as es:
        npre = NPRE if pre is not None else 0
        ngr = NFC // 2 - npre
        w_in = sb(nc, es, "w_in" + tag, [128, 8, 2 * ngr * 256], BF16)
        w_out = sb(nc, es, "w_out" + tag, [128, NFC, D], BF16)
        w_in_v = w_in_d.rearrange("(k p) n -> p k n", p=128)
        win_t = {}
        wsrc = {}
        for gi in range(NFC // 2):
            for half in range(2):
                c0 = half * DFF + gi * 256
                if gi < npre:
                    t, o0 = pre[1][(half, gi)]
                    wsrc[(half, gi)] = (pre[0], o0 - c0)
                else:
                    o0 = (half * ngr + gi - npre) * 256
                    t = Tl(w_in[:, :, o0:o0 + 256])
                    P.dma("pool", t.ap, w_in_v[:, :, c0:c0 + 256], [], [t])
                    wsrc[(half, gi)] = (w_in, o0 - c0)
                win_t[(half, gi)] = t
        w_out_v = w_out_d.rearrange("(c p) n -> p c n", p=128)
        wout_t = []
        for gi in range(NFC // 2):
            t = Tl(w_out[:, 2 * gi:2 * gi + 2, :])
            P.dma("pool", t.ap, w_out_v[:, 2 * gi:2 * gi + 2, :], [], [t])
            wout_t.append(t)
        g_bc = load_bcast(P, es, "g_bc" + tag, g_d, D)
        b_bc = load_bcast(P, es, "b_bc" + tag, b_d, D)

        xl = [Tl(sb(nc, es, "xl%d%s" % (i, tag), [128, D], F32)[:, :]) for i in range(2)]
        xb = [Tl(sb(nc, es, "xb%d%s" % (i, tag), [128, D], BF16)[:, :]) for i in range(2)]
        xr = [Tl(sb(nc, es, "xr%d%s" % (i, tag), [128, D], F32)[:, :]) for i in range(2)]
        xT = [sb(nc, es, "xT%d%s" % (i, tag), [128, 8, 512], BF16) for i in range(2)]
        xT_t = [[Tl(xT[i][:, :, t * 128:(t + 1) * 128]) for t in range(4)] for i in range(2)]
        actT = sb(nc, es, "actT" + tag, [128, NFC, 512], BF16)
        act_t = [Tl(actT[:, c, :]) for c in range(NFC)]
        sg = [Tl(sb(nc, es, "sg%d%s" % (i, tag), [128, 512], F32)[:, :]) for i in range(2)]
        st = [Tl(sb(nc, es, "st%d%s" % (i, tag), [128, 12], F32)[:, :]) for i in range(2)]
        mv = [Tl(sb(nc, es, "mv%d%s" % (i, tag), [128, 2], F32)[:, :]) for i in range(2)]
        rs = [Tl(sb(nc, es, "rs%d%s" % (i, tag), [128, 1], F32)[:, :]) for i in range(2)]
        nm = [Tl(sb(nc, es, "nm%d%s" % (i, tag), [128, 1], F32)[:, :]) for i in range(2)]
        tp = [Tl(ps(nc, es, "tp%d%s" % (i, tag), [128, D], BF16)[:, :]) for i in range(2)]
        hg = [Tl(ps(nc, es, "hg%d%s" % (i, tag), [128, 512], F32)[:, :]) for i in range(2)]
        hu = [Tl(ps(nc, es, "hu%d%s" % (i, tag), [128, 512], F32)[:, :]) for i in range(2)]
        po = [Tl(ps(nc, es, "po%d%s" % (i, tag), [128, 512], F32)[:, :]) for i in range(2)]

        cfac = 0.5 / ALPHA
        nblk = NT // 4
        def prep_x(bb, t):
            ti = 4 * bb + t
            load_x_transposed(P, C, Xin[ti], xl[ti % 2], xb[ti % 2], tp[ti % 2],
                              xT[bb % 2][:, :, t * 128:(t + 1) * 128], xT_t[bb % 2][t])

        for t in range(4):
            prep_x(0, t)
        for b in range(nblk):
            sbk = b % 2
            for c in range(NFC):
                if b + 1 < nblk and c in (2, 7, 12, 17):
                    prep_x(b + 1, (c - 2) // 5)
                s2 = c % 2
                for (half, acc) in ((0, hg[s2]), (1, hu[s2])):
                    wt = win_t[(half, c // 2)]
                    wten, woff = wsrc[(half, c // 2)]
                    col = half * DFF + c * 128 + woff
                    for k in range(8):
                        P.mm(acc.ap, wten[:, k, col:col + 128], xT[sbk][:, k, :], k == 0, k == 7,
                             [wt] + xT_t[sbk], [acc])
                P.act(sg[s2].ap, hg[s2].ap, AF.Silu, [hg[s2]], [sg[s2]])
                P.v("dve", "tensor_tensor", [sg[s2], hu[s2]], [act_t[c]], act_t[c].ap, sg[s2].ap, hu[s2].ap, ALU.mult)
            for t in range(4):
                ti = 4 * b + t
                r = xr[ti % 2]
                P.dma("sp", r.ap, Xin[ti].ap, [Xin[ti]], [r])
                for n in range(2):
                    acc = po[n]
                    for c in range(NFC):
                        P.mm(acc.ap, actT[:, c, t * 128:(t + 1) * 128], w_out[:, c, n * 512:(n + 1) * 512],
                             c == 0, c == NFC - 1, [act_t[c], wout_t[c // 2]], [acc])
                    P.v("dve", "scalar_tensor_tensor", [acc, r], [r], r.ap[:, n * 512:(n + 1) * 512], acc.ap, cfac,
                        r.ap[:, n * 512:(n + 1) * 512], ALU.mult, ALU.add)
                post_norm_tile(P, C, r, g_bc, b_bc, Xout[ti], st[ti % 2], mv[ti % 2], rs[ti % 2], nm[ti % 2])
        P.barrier()


def phase_ple(P, C, Xin, Xout, Pt, wg_d, bg_d, wp_d, g_d, b_d, tag):
    nc = P.nc
    with ExitStack() as es:
        wg = Tl(sb(nc, es, "wg" + tag, [128, 8, D], BF16)[:, :, :])
        wp = Tl(sb(nc, es, "wp" + tag, [128, 2, D], BF16)[:, :, :])
        P.dma("pool", wg.ap, wg_d.rearrange("(k p) n -> p k n", p=128), [], [wg])
        P.dma("pool", wp.ap, wp_d.rearrange("(k p) n -> p k n", p=128), [], [wp])
        bg_bc = load_bcast(P, es, "bg_bc" + tag, bg_d, D)
        g_bc = load_bcast(P, es, "g_bc" + tag, g_d, D)
        b_bc = load_bcast(P, es, "b_bc" + tag, b_d, D)
        xl = [Tl(sb(nc, es, "xl%d%s" % (i, tag), [128, D], F32)[:, :]) for i in range(2)]
        xb = [Tl(sb(nc, es, "xb%d%s" % (i, tag), [128, D], BF16)[:, :]) for i in range(2)]
        pl = [Tl(sb(nc, es, "pl%d%s" % (i, tag), [128, PLE], F32)[:, :]) for i in range(2)]
        pbf = [Tl(sb(nc, es, "pbf%d%s" % (i, tag), [128, PLE], BF16)[:, :]) for i in range(2)]
        xr = [Tl(sb(nc, es, "xr%d%s" % (i, tag), [128, D], F32)[:, :]) for i in range(4)]
        xT = [sb(nc, es, "xT%d%s" % (i, tag), [128, 8, 512], BF16) for i in range(2)]
        xT_t = [[Tl(xT[i][:, :, t * 128:(t + 1) * 128]) for t in range(4)] for i in range(2)]
        pT = [sb(nc, es, "pT%d%s" % (i, tag), [128, 2, 512], BF16) for i in range(2)]
        pT_t = [[Tl(pT[i][:, :, t * 128:(t + 1) * 128]) for t in range(4)] for i in range(2)]
        tmp = [Tl(sb(nc, es, "tmp%d%s" % (i, tag), [128, 512], F32)[:, :]) for i in range(2)]
        st = [Tl(sb(nc, es, "st%d%s" % (i, tag), [128, 12], F32)[:, :]) for i in range(2)]
        mv = [Tl(sb(nc, es, "mv%d%s" % (i, tag), [128, 2], F32)[:, :]) for i in range(2)]
        rs = [Tl(sb(nc, es, "rs%d%s" % (i, tag), [128, 1], F32)[:, :]) for i in range(2)]
        nm = [Tl(sb(nc, es, "nm%d%s" % (i, tag), [128, 1], F32)[:, :]) for i in range(2)]
        tp = [Tl(ps(nc, es, "tp%d%s" % (i, tag), [128, D], BF16)[:, :]) for i in range(2)]
        tpp = Tl(ps(nc, es, "tpp" + tag, [128, PLE], BF16)[:, :])
        pg = [Tl(ps(nc, es, "pg%d%s" % (i, tag), [128, 512], F32)[:, :]) for i in range(2)]
        pp = [Tl(ps(nc, es, "pp%d%s" % (i, tag), [128, 512], F32)[:, :]) for i in range(2)]
        def prep(bb, t):
            ti = 4 * bb + t
            sk = bb % 2
            load_x_transposed(P, C, Xin[ti], xl[ti % 2], xb[ti % 2], tp[ti % 2],
                              xT[sk][:, :, t * 128:(t + 1) * 128], xT_t[sk][t])
            P.dma("sp", pl[ti % 2].ap, Pt[ti].ap, [Pt[ti]], [pl[ti % 2]])
            P.act(pbf[ti % 2].ap, pl[ti % 2].ap, AF.Copy, [pl[ti % 2]], [pbf[ti % 2]])
            for k in range(2):
                P.tr(tpp.ap[:, k * 128:(k + 1) * 128], pbf[ti % 2].ap[:, k * 128:(k + 1) * 128], C.ident.ap,
                     [pbf[ti % 2], C.ident], [tpp])
            P.v("dve", "tensor_copy", [tpp], [pT_t[sk][t]], pT[sk][:, :, t * 128:(t + 1) * 128],
                tpp.ap.rearrange("p (k t) -> p k t", k=2))

        for t in range(4):
            prep(0, t)
        for b in range(NT // 4):
            sbk = b % 2
            for t in range(4):
                ti = 4 * b + t
                r = xr[t]
                if b + 1 < NT // 4:
                    prep(b + 1, t)
                P.dma("sp", r.ap, Xin[ti].ap, [Xin[ti]], [r])
                for n in range(2):
                    cs = slice(n * 512, (n + 1) * 512)
                    for k in range(8):
                        P.mm(pg[n].ap, xT[sbk][:, k, t * 128:(t + 1) * 128], wg.ap[:, k, cs], k == 0, k == 7,
                             [xT_t[sbk][t], wg], [pg[n]])
                    for k in range(2):
                        P.mm(pp[n].ap, pT[sbk][:, k, t * 128:(t + 1) * 128], wp.ap[:, k, cs], k == 0, k == 1,
                             [pT_t[sbk][t], wp], [pp[n]])
                    tm = tmp[n]
                    P.v("dve", "tensor_tensor", [pg[n], bg_bc], [tm], tm.ap, pg[n].ap, bg_bc.ap[:, cs], ALU.add)
                    P.act(tm.ap, tm.ap, AF.Sigmoid, [tm], [tm])
                    P.v("dve", "tensor_tensor", [tm, pp[n]], [tm], tm.ap, tm.ap, pp[n].ap, ALU.mult)
                    P.v("dve", "scalar_tensor_tensor", [tm, r], [r], r.ap[:, cs], tm.ap, 1.0 / ALPHA, r.ap[:, cs],
                        ALU.mult, ALU.add)
            for t in range(4):
                ti = 4 * b + t
                post_norm_tile(P, C, xr[t], g_bc, b_bc, Xout[ti], st[ti % 2], mv[ti % 2], rs[ti % 2], nm[ti % 2])
        P.barrier()


def phase_attout(P, C, Xin, Xout, OTv, OT_reads, nch, wo_d, g_d, b_d, tag):
    nc = P.nc
    ng = 4
    cpg = nch // ng
    with ExitStack() as es:
        wo = sb(nc, es, "wo" + tag, [128, nch, D], BF16)
        wo_v = wo_d.rearrange("(c d) n -> d c n", d=128)
        wo_t = []
        for q in range(ng):
            t = Tl(wo[:, q * cpg:(q + 1) * cpg, :])
            P.dma("pool", t.ap, wo_v[:, q * cpg:(q + 1) * cpg, :], [], [t])
            wo_t.append(t)
        g_bc = load_bcast(P, es, "g_bc" + tag, g_d, D)
        b_bc = load_bcast(P, es, "b_bc" + tag, b_d, D)
        ob = [sb(nc, es, "ob%d%s" % (i, tag), [128, nch, 512], BF16) for i in range(2)]
        ob_t = [[Tl(ob[i][:, q * cpg:(q + 1) * cpg, :]) for q in range(ng)] for i in range(2)]
        xr = [Tl(sb(nc, es, "xr%d%s" % (i, tag), [128, D], F32)[:, :]) for i in range(4)]
        st = [Tl(sb(nc, es, "st%d%s" % (i, tag), [128, 12], F32)[:, :]) for i in range(2)]
        mv = [Tl(sb(nc, es, "mv%d%s" % (i, tag), [128, 2], F32)[:, :]) for i in range(2)]
        rs = [Tl(sb(nc, es, "rs%d%s" % (i, tag), [128, 1], F32)[:, :]) for i in range(2)]
        nm = [Tl(sb(nc, es, "nm%d%s" % (i, tag), [128, 1], F32)[:, :]) for i in range(2)]
        po = [Tl(ps(nc, es, "po%d%s" % (i, tag), [128, 512], F32)[:, :]) for i in range(4)]
        def load_ob(bb):
            for q in range(ng):
                rd = []
                for c in range(q * cpg, (q + 1) * cpg):
                    rd += OT_reads(c, bb)
                P.dma("sp", ob_t[bb % 2][q].ap,
                      OTv[q * cpg:(q + 1) * cpg, :, bb * 512:(bb + 1) * 512].rearrange("c d t -> d c t"), rd, [ob_t[bb % 2][q]])

        load_ob(0)
        for b in range(NT // 4):
            sl = b % 2
            if b + 1 < NT // 4:
                load_ob(b + 1)
            for t in range(4):
                ti = 4 * b + t
                r = xr[ti % 4]
                P.dma("sp", r.ap, Xin[ti].ap, [Xin[ti]], [r])
                for n in range(2):
                    cs = slice(n * 512, (n + 1) * 512)
                    acc = po[(2 * ti + n) % 4]
                    for c in range(nch):
                        P.mm(acc.ap, ob[sl][:, c, t * 128:(t + 1) * 128], wo[:, c, cs], c == 0, c == nch - 1,
                             [ob_t[sl][c // cpg], wo_t[c // cpg]], [acc])
                    P.v("dve", "scalar_tensor_tensor", [acc, r], [r], r.ap[:, cs], acc.ap, 1.0 / ALPHA, r.ap[:, cs],
                        ALU.mult, ALU.add)
                post_norm_tile(P, C, r, g_bc, b_bc, Xout[ti], st[ti % 2], mv[ti % 2], rs[ti % 2], nm[ti % 2])
        P.barrier()


def phase_latent(P, C, Xin, w_d, ncol, gn_d, LT, LT_t, tag, rope_w_d=None, KRT=None, KRT_t=None):
    nc = P.nc
    nch = ncol // 128
    with ExitStack() as es:
        w = Tl(sb(nc, es, "wl" + tag, [128, 8, ncol], BF16)[:, :, :])
        P.dma("pool", w.ap, w_d.rearrange("(k p) n -> p k n", p=128)[:, :, 0:ncol], [], [w])
        gn_bc = load_bcast(P, es, "gn_bc" + tag, gn_d, ncol)
        eps_r = Tl(sb(nc, es, "eps_r" + tag, [128, 1], F32)[:, :])
        P.v("dve", "memset", [], [eps_r], eps_r.ap, RMS_EPS)
        if rope_w_d is not None:
            phase_consts(P, C, es, ["cosT", "sinT"], tag)
            wr = Tl(sb(nc, es, "wr" + tag, [128, 8, 64], BF16)[:, :, :])
            wrs = Tl(sb(nc, es, "wrs" + tag, [128, 8, 64], BF16)[:, :, :])
            rv = rope_w_d.rearrange("(k p) n -> p k n", p=128)
            P.dma("pool", wr.ap, rv[:, :, ncol:ncol + 64], [], [wr])
            P.dma("pool", wrs.ap[:, :, 0:32], rv[:, :, ncol + 32:ncol + 64], [], [wrs])
            P.dma("pool", wrs.ap[:, :, 32:64], rv[:, :, ncol:ncol + 32], [], [wrs])
            pa = Tl(ps(nc, es, "pa" + tag, [64, 512], F32)[:, :])
            pb_ = Tl(ps(nc, es, "pb" + tag, [64, 512], F32)[:, :])
            t1 = Tl(sb(nc, es, "t1" + tag, [64, 512], F32)[:, :])
            t2 = Tl(sb(nc, es, "t2" + tag, [64, 512], F32)[:, :])
            kr = [Tl(sb(nc, es, "kr%d%s" % (i, tag), [64, 512], BF16)[:, :]) for i in range(2)]
        xl = [Tl(sb(nc, es, "xl%d%s" % (i, tag), [128, D], F32)[:, :]) for i in range(2)]
        xb = [Tl(sb(nc, es, "xb%d%s" % (i, tag), [128, D], BF16)[:, :]) for i in range(2)]
        xT = [sb(nc, es, "xT%d%s" % (i, tag), [128, 8, 512], BF16) for i in range(2)]
        xT_t = [[Tl(xT[i][:, :, t * 128:(t + 1) * 128]) for t in range(4)] for i in range(2)]
        junk = Tl(sb(nc, es, "junk" + tag, [128, ncol], F32)[:, :])
        ss = [Tl(sb(nc, es, "ss%d%s" % (i, tag), [128, 1], F32)[:, :]) for i in range(2)]
        rs = [Tl(sb(nc, es, "rs%d%s" % (i, tag), [128, 1], F32)[:, :]) for i in range(2)]
        cb = [Tl(sb(nc, es, "cb%d%s" % (i, tag), [128, ncol], BF16)[:, :]) for i in range(2)]
        lt = [sb(nc, es, "lt%d%s" % (i, tag), [128, nch, 512], BF16) for i in range(2)]
        lt_t = [[Tl(lt[i][:, :, t * 128:(t + 1) * 128]) for t in range(4)] for i in range(2)]
        tp = [Tl(ps(nc, es, "tp%d%s" % (i, tag), [128, D], BF16)[:, :]) for i in range(2)]
        hk = [Tl(ps(nc, es, "hk%d%s" % (i, tag), [128, 512], F32)[:, :]) for i in range(2)]
        tp2 = Tl(ps(nc, es, "tp2" + tag, [128, ncol], BF16)[:, :])
        def prep(bb, t):
            ti = 4 * bb + t
            load_x_transposed(P, C, Xin[ti], xl[ti % 2], xb[ti % 2], tp[ti % 2],
                              xT[bb % 2][:, :, t * 128:(t + 1) * 128], xT_t[bb % 2][t])

        for t in range(4):
            prep(0, t)
        for b in range(NT // 4):
            sbk = b % 2
            for t in range(4):
                ti = 4 * b + t
                if b + 1 < NT // 4:
                    prep(b + 1, t)
                h = hk[ti % 2]
                for k in range(8):
                    P.mm(h.ap[:, 0:ncol], xT[sbk][:, k, t * 128:(t + 1) * 128], w.ap[:, k, :], k == 0, k == 7,
                         [xT_t[sbk][t], w], [h])
                s_ = ss[ti % 2]
                r_ = rs[ti % 2]
                P.v("dve", "memset", [], [s_], s_.ap, 0.0)
                P.act(junk.ap, h.ap[:, 0:ncol], AF.Square, [h, s_], [junk, s_], accum_out=s_.ap)
                P.act(r_.ap, s_.ap, AF.Sqrt, [s_, eps_r], [r_], bias=eps_r.ap, scale=1.0 / ncol)
                P.v("dve", "reciprocal", [r_], [r_], r_.ap, r_.ap)
                c_ = cb[ti % 2]
                P.v("dve", "scalar_tensor_tensor", [h, r_, gn_bc], [c_], c_.ap, h.ap[:, 0:ncol], r_.ap, gn_bc.ap,
                    ALU.mult, ALU.mult)
                for k in range(nch):
                    P.tr(tp2.ap[:, k * 128:(k + 1) * 128], c_.ap[:, k * 128:(k + 1) * 128], C.ident.ap,
                         [c_, C.ident], [tp2])
                P.v("dve", "tensor_copy", [tp2], [lt_t[sbk][t]], lt[sbk][:, :, t * 128:(t + 1) * 128],
                    tp2.ap.rearrange("p (k t) -> p k t", k=nch))
            P.dma("pool", LT[:, :, b * 512:(b + 1) * 512].rearrange("c p t -> p c t"), lt[sbk][:, :, :], lt_t[sbk], [LT_t[b]])
            if rope_w_d is not None:
                cs = slice(b * 512, (b + 1) * 512)
                for k in range(8):
                    P.mm(pa.ap, wr.ap[:, k, :], xT[sbk][:, k, :], k == 0, k == 7, [wr] + xT_t[sbk], [pa])
                for k in range(8):
                    P.mm(pb_.ap, wrs.ap[:, k, :], xT[sbk][:, k, :], k == 0, k == 7, [wrs] + xT_t[sbk], [pb_])
                P.v("dve", "tensor_tensor", [pa, C.cosT], [t1], t1.ap, pa.ap, C.cosT.ap[:, cs], ALU.mult)
                P.v("dve", "tensor_tensor", [pb_, C.sinT], [t2], t2.ap, pb_.ap, C.sinT.ap[:, cs], ALU.mult)
                P.v("dve", "tensor_tensor", [t1, t2], [kr[sbk]], kr[sbk].ap, t1.ap, t2.ap, ALU.add)
                P.dma("pool", KRT[:, cs], kr[sbk].ap, [kr[sbk]], [KRT_t[b]])
        P.barrier()


def load_const(P, es, name, d_ap, shape, dt):
    h = sb(P.nc, es, name, shape, dt)
    t = Tl(h[tuple(slice(None) for _ in shape)])
    P.dma("sp", t.ap, d_ap, [], [t])
    return t


def phase_consts(P, C, es, names, tag):
    for n in names:
        d_ap, shape, dt = C.d[n]
        setattr(C, n, load_const(P, es, "k_" + n + tag, d_ap, shape, dt))


def drive_attention(tiles, LA, stageA, stageB, epilogue, items_for, n_units, blocks_per_unit, defer=6):
    for it in items_for(0):
        it()
    n_in_unit = {}
    for t in tiles:
        t["k"] = n_in_unit.get(t["u"], 0)
        n_in_unit[t["u"]] = t["k"] + 1
    pending, total, popped = {}, {}, {}
    deferred = []
    n = len(tiles)
    for i in range(n + LA):
        if i < n:
            stageA(i, tiles[i])
        j = i - LA
        if j < 0:
            continue
        t = tiles[j]
        stageB(j, t)
        if t["last"]:
            d = epilogue(t)
            if d is not None:
                deferred.append((j + defer, d))
        while deferred and deferred[0][0] <= j:
            deferred.pop(0)[1]()
        u = t["u"]
        if u + 1 < n_units:
            if u + 1 not in pending:
                pending[u + 1] = items_for(u + 1)
                total[u + 1] = len(pending[u + 1])
                popped[u + 1] = 0
            q = pending[u + 1]
            target = min(total[u + 1], int((t["k"] + 1) * total[u + 1] / (0.85 * n_in_unit[u])) + 1)
            while popped[u + 1] < target and q:
                q.pop(0)()
                popped[u + 1] += 1
    for _, d in deferred:
        d()


def diag_range(kt, b):
    i = kt - 4 * b
    c0 = max(i, 0) * 128
    return i, c0


def phase_fox(P, C, Xin, w_in_d, bf_d, OT, OT_t, tag="fx"):
    nc = P.nc
    scale = FOX_D ** -0.5
    w_v = w_in_d.rearrange("(k p) n -> p k n", p=128)
    with ExitStack() as es:
        phase_consts(P, C, es, ["tri", "Lf", "E127f", "E127b", "sel", "onesf"], tag)
        xTa = sb(nc, es, "xTa", [128, 8, S], BF16)
        xTa_t = [Tl(xTa[:, :, t * 128:(t + 1) * 128]) for t in range(NT)]
        wf = Tl(sb(nc, es, "wf", [128, 8, 16], BF16)[:, :, :])
        P.dma("pool", wf.ap, w_v[:, :, 3 * D:3 * D + 16], [], [wf])
        bf_bc = load_bcast(P, es, "bf_bc", bf_d, 16)
        logf = sb(nc, es, "logf", [128, NT, 16], F32)
        logf_t = [Tl(logf[:, t, :]) for t in range(NT)]
        cum = sb(nc, es, "cum", [128, NT, 16], F32)
        cum_t = [Tl(cum[:, t, :]) for t in range(NT)]
        ncum = sb(nc, es, "ncum", [128, NT, 16], F32)
        ncum_t = [Tl(ncum[:, t, :]) for t in range(NT)]
        cumb = sb(nc, es, "cumb", [128, NT, 16], BF16)
        cumb_t = [Tl(cumb[:, t, :]) for t in range(NT)]
        Cq = sb(nc, es, "Cq", [16, S], BF16)
        Cq_t = [Tl(Cq[:, g * 512:(g + 1) * 512]) for g in range(8)]
        wpair = [sb(nc, es, "wpair%d" % i, [128, 8, 384], BF16) for i in range(2)]
        wpair_t = [Tl(wpair[i][:, :, :]) for i in range(2)]
        qT = [sb(nc, es, "qT%d" % i, [128, 2, S], BF16) for i in range(2)]
        kT = [sb(nc, es, "kT%d" % i, [128, S], BF16) for i in range(2)]
        vS = [sb(nc, es, "vS%d" % i, [128, NT, 2, 65], BF16) for i in range(2)]
        qT_t = [[Tl(qT[i][:, :, b * 512:(b + 1) * 512]) for b in range(8)] for i in range(2)]
        kT_t = [[Tl(kT[i][:, b * 512:(b + 1) * 512]) for b in range(8)] for i in range(2)]
        vS_t = [[Tl(vS[i][:, 4 * b:4 * b + 4, :, :]) for b in range(8)] for i in range(2)]
        pT = [Tl(sb(nc, es, "pT%d" % i, [128, 512], BF16)[:, :]) for i in range(6)]
        oun = [Tl(sb(nc, es, "oun%d" % i, [64, 512], F32)[:, :]) for i in range(2)]
        rr = [Tl(sb(nc, es, "rr%d" % i, [65, 512], F32)[:, :]) for i in range(2)]
        oo = [Tl(sb(nc, es, "oo%d" % i, [64, 512], BF16)[:, :]) for i in range(2)]
        with ExitStack() as es1:
            xl = [Tl(sb(nc, es1, "xl%d%s" % (i, tag), [128, D], F32)[:, :]) for i in range(2)]
            xb = [Tl(sb(nc, es1, "xb%d%s" % (i, tag), [128, D], BF16)[:, :]) for i in range(2)]
            z = [Tl(sb(nc, es1, "z%d" % i, [128, 16], F32)[:, :]) for i in range(2)]
            za = [Tl(sb(nc, es1, "za%d" % i, [128, 16], F32)[:, :]) for i in range(2)]
            zm = [Tl(sb(nc, es1, "zm%d" % i, [128, 16], F32)[:, :]) for i in range(2)]
            tp = [Tl(ps(nc, es1, "tp%d%s" % (i, tag), [128, D], BF16)[:, :]) for i in range(2)]
            fl = [Tl(ps(nc, es1, "fl%d" % i, [128, 16], F32)[:, :]) for i in range(2)]
            cps = [Tl(ps(nc, es1, "cps%d" % i, [128, 16], F32)[:, :]) for i in range(2)]
            cqp = [Tl(ps(nc, es1, "cqp%d" % i, [16, 512], F32)[:, :]) for i in range(2)]
            for ti in range(NT):
                s = ti % 2
                load_x_transposed(P, C, Xin[ti], xl[s], xb[s], tp[s], xTa[:, :, ti * 128:(ti + 1) * 128], xTa_t[ti])
                for k in range(8):
                    P.mm(fl[s].ap, xTa[:, k, ti * 128:(ti + 1) * 128], wf.ap[:, k, :], k == 0, k == 7, [xTa_t[ti], wf], [fl[s]])
                P.v("dve", "tensor_tensor", [fl[s], bf_bc], [z[s]], z[s].ap, fl[s].ap, bf_bc.ap, ALU.add)
                if "abs" not in FOX_SKIP:
                    P.act(za[s].ap, z[s].ap, AF.Abs, [z[s]], [za[s]])
                else:
                    P.act(za[s].ap, z[s].ap, AF.Copy, [z[s]], [za[s]])
                if "exp" not in FOX_SKIP:
                    P.act(za[s].ap, za[s].ap, AF.Exp, [za[s]], [za[s]], scale=-1.0)
                if "ln" not in FOX_SKIP:
                    P.act(za[s].ap, za[s].ap, AF.Ln, [za[s]], [za[s]], bias=1.0)
                P.v("dve", "tensor_scalar_min", [z[s]], [zm[s]], zm[s].ap, z[s].ap, 0.0)
                P.v("dve", "tensor_sub", [zm[s], za[s]], [logf_t[ti]], logf_t[ti].ap, zm[s].ap, za[s].ap)
                c = cps[s]
                P.mm(c.ap, C.Lf.ap, logf_t[ti].ap, True, ti == 0, [C.Lf, logf_t[ti]], [c])
                if ti > 0:
                    P.mm(c.ap, C.E127f.ap, cum_t[ti - 1].ap, False, True, [C.E127f, cum_t[ti - 1]], [c])
                P.act(cum_t[ti].ap, c.ap, AF.Copy, [c], [cum_t[ti]])
                P.v("dve", "tensor_scalar_mul", [c], [ncum_t[ti]], ncum_t[ti].ap, c.ap, -1.0)
                P.v("dve", "tensor_copy", [c], [cumb_t[ti]], cumb_t[ti].ap, c.ap)
            for g in range(8):
                cq = cqp[g % 2]
                for j in range(4):
                    ti = 4 * g + j
                    P.mm(cq.ap[:, j * 128:(j + 1) * 128], cumb_t[ti].ap, C.E127b.ap, True, True, [cumb_t[ti], C.E127b], [cq])
                P.v("dve", "tensor_scalar_mul", [cq], [Cq_t[g]], Cq_t[g].ap, cq.ap, 1.0 / scale)
            P.barrier()
        sc = [Tl(ps(nc, es, "sc%d" % i, [128, 512], F32)[:, :]) for i in range(3)]
        oacc = [Tl(ps(nc, es, "oacc%d" % i, [65, 512], F32)[:, :]) for i in range(2)]
        bc = Tl(ps(nc, es, "bc", [128, 512], F32)[:, :])
        srow = [Tl(sb(nc, es, "srow%d" % i, [65, 512], F32)[:, :]) for i in range(2)]
        cqbc = [Tl(sb(nc, es, "cqbc%d" % i, [128, 512], F32)[:, :]) for i in range(2)]
        tmpS = [Tl(sb(nc, es, "tmpS%d" % i, [128, 512], F32)[:, :]) for i in range(5)]
        pj = [Tl(ps(nc, es, "pj%d" % i, [128, 512], F32)[:, :]) for i in range(2)]
        pjc = [0]
        for i in range(2):
            P.v("dve", "memset", [], vS_t[i], vS[i][:, :, :, 64:65], 1.0)
            P.v("pool", "memset", [], qT_t[i], qT[i][64:128, 0, :], 0.0)
            P.v("pool", "memset", [], qT_t[i], qT[i][0:64, 1, :], 0.0)

        def proj_items(g):
            sl = g % 2
            items = []

            def load_w():
                for j in range(3):
                    c0 = j * D + g * 128
                    P.dma("pool", wpair[sl][:, :, j * 128:(j + 1) * 128], w_v[:, :, c0:c0 + 128], [], [wpair_t[sl]])

            def qk_item(j, b):
                dst, dst_t = (qT, qT_t) if j == 0 else (kT, kT_t)

                def f():
                    acc = pj[pjc[0] % len(pj)]
                    pjc[0] += 1
                    for k in range(8):
                        P.mm(acc.ap, wpair[sl][:, k, j * 128:(j + 1) * 128], xTa[:, k, b * 512:(b + 1) * 512],
                             k == 0, k == 7, [wpair_t[sl]] + xTa_t[4 * b:4 * b + 4], [acc])
                    if j == 0:
                        P.act(qT[sl][0:64, 0, b * 512:(b + 1) * 512], acc.ap[0:64, :], AF.Copy, [acc], [qT_t[sl][b]])
                        P.act(qT[sl][64:128, 1, b * 512:(b + 1) * 512], acc.ap[64:128, :], AF.Copy, [acc], [qT_t[sl][b]])
                    else:
                        P.act(dst[sl][:, b * 512:(b + 1) * 512], acc.ap, AF.Copy, [acc], [dst_t[sl][b]])
                return f

            def v_item(b):
                def f():
                    acc = pj[pjc[0] % len(pj)]
                    pjc[0] += 1
                    for t in range(4):
                        ti = 4 * b + t
                        for k in range(8):
                            P.mm(acc.ap[:, t * 128:(t + 1) * 128], xTa[:, k, ti * 128:(ti + 1) * 128],
                                 wpair[sl][:, k, 256:384], k == 0, k == 7, [wpair_t[sl], xTa_t[ti]], [acc])
                    P.act(vS[sl][:, 4 * b:4 * b + 4, :, 0:64], acc.ap.rearrange("p (t h d) -> p t h d", t=4, h=2), AF.Copy,
                          [acc], [vS_t[sl][b]])
                return f

            items.append(load_w)
            for b in range(8):
                items.append(qk_item(0, b))
                items.append(qk_item(1, b))
                items.append(v_item(b))
            return items

        def stageA(i, t):
            g, hh, b, kt = t["u"], t["hh"], t["b"], t["kt"]
            sl = g % 2
            h = 2 * g + hh
            pb = 64 * hh
            di, c0 = diag_range(kt, b)
            qs = slice(b * 512 + c0, (b + 1) * 512)
            s_ = sc[i % len(sc)]
            p_ = pT[i % len(pT)]
            cb_ = cqbc[t["pc"] % 2]
            if kt == 0:
                P.mm(bc.ap, C.sel.ap[:, h, :], Cq[:, b * 512:(b + 1) * 512], True, True, [C.sel, Cq_t[b]], [bc])
                P.act(cb_.ap, bc.ap, AF.Copy, [bc], [cb_])
            tm = tmpS[i % len(tmpS)]
            P.mm(s_.ap[:, c0:512], kT[sl][:, kt * 128:(kt + 1) * 128], qT[sl][:, hh, qs], True, True,
                 [kT_t[sl][kt // 4], qT_t[sl][b]], [s_])
            P.v("dve", "tensor_tensor", [s_, cb_], [tm], tm.ap[:, c0:512], s_.ap[:, c0:512], cb_.ap[:, c0:512], ALU.add)
            P.act(p_.ap[:, c0:512], tm.ap[:, c0:512], AF.Exp, [tm, ncum_t[kt]], [p_], bias=ncum[:, kt, h:h + 1], scale=scale)
            if di >= 0:
                P.v("pool", "tensor_tensor", [p_, C.tri], [p_], p_.ap[:, c0:c0 + 128], p_.ap[:, c0:c0 + 128], C.tri.ap, ALU.mult)

        def stageB(i, t):
            g, hh, b, kt, pc = t["u"], t["hh"], t["b"], t["kt"], t["pc"]
            sl = g % 2
            di, c0 = diag_range(kt, b)
            p_ = pT[i % len(pT)]
            oa = oacc[pc % len(oacc)]
            P.mm(oa.ap[:, c0:512], vS[sl][:, kt, hh, :], p_.ap[:, c0:512], kt == 0, kt == 4 * b + 3,
                 [vS_t[sl][kt // 4], p_], [oa])

        def epilogue(t):
            g, hh, b, pc = t["u"], t["hh"], t["b"], t["pc"]
            h = 2 * g + hh
            oa = oacc[pc % len(oacc)]
            u = oun[pc % 2]
            r_ = rr[pc % 2]
            o_ = oo[pc % 2]
            P.v("dve", "tensor_copy", [oa], [u], u.ap, oa.ap[0:64, :])
            sr = srow[pc % 2]
            P.act(sr.ap[64:65, :], oa.ap[64:65, :], AF.Ln, [oa], [sr])
            P.act(r_.ap[64:65, :], sr.ap[64:65, :], AF.Exp, [sr], [r_], scale=-1.0)

            def late():
                P.mm(bc.ap[0:64, :], C.onesf.ap[64:65, 0:64], r_.ap[64:65, :], True, True, [C.onesf, r_], [bc])
                P.v("dve", "tensor_tensor", [u, bc], [o_], o_.ap, u.ap, bc.ap[0:64, :], ALU.mult)
                P.dma("sp", OT[h, :, b * 512:(b + 1) * 512], o_.ap, [o_], [OT_t[h][b]])
            return late

        tiles = []
        pc = 0
        for g in range(FOX_PAIRS):
            for hh in range(2):
                for b in range(8):
                    for kt in range(4 * b + 4):
                        tiles.append(dict(u=g, hh=hh, b=b, kt=kt, pc=pc, last=(kt == 4 * b + 3), bi=hh * 8 + b))
                    pc += 1
        drive_attention(tiles, ATT_LA, stageA, stageB, epilogue, proj_items, FOX_PAIRS, 16)
        P.barrier()


def phase_mla(P, C, CQT, CQT_t, CKVT, CKVT_t, KRT, KRT_t, wuq_d, wup_d, OT, OT_t, tag="ml"):
    nc = P.nc
    scale = (NOPE + ROPE) ** -0.5
    uq_v = wuq_d.rearrange("(c p) h e -> p c h e", p=128)
    up_v = wup_d.rearrange("(c p) h e -> p c h e", p=128)
    with ExitStack() as es:
        phase_consts(P, C, es, ["tri", "onesf", "cosT", "sinT"], tag)
        cqs = sb(nc, es, "cqA", [128, 3, S], BF16)
        ckvs = sb(nc, es, "ckvA", [128, 2, S], BF16)
        krts = sb(nc, es, "krA", [128, S], BF16)
        cq_t, ckv_t, krt_t = [], [], []
        for b in range(8):
            cs = slice(b * 512, (b + 1) * 512)
            cq_t.append(Tl(cqs[:, :, cs]))
            ckv_t.append(Tl(ckvs[:, :, cs]))
            krt_t.append(Tl(krts[:, cs]))
            P.dma("sp", ckv_t[b].ap, CKVT[:, :, cs].rearrange("c p t -> p c t"), [CKVT_t[b]], [ckv_t[b]])
            P.dma("sp", cq_t[b].ap, CQT[:, :, cs].rearrange("c p t -> p c t"), [CQT_t[b]], [cq_t[b]])
            P.v("pool", "memset", [], [krt_t[b]], krts[64:128, cs], 0.0)
            P.dma("sp", krts[0:64, cs], KRT[:, cs], [KRT_t[b]], [krt_t[b]])
        wq = [Tl(sb(nc, es, "wq%d" % i, [128, 3, 192], BF16)[:, :, :]) for i in range(2)]
        wqs = [Tl(sb(nc, es, "wqs%d" % i, [128, 3, 64], BF16)[:, :, :]) for i in range(2)]
        wkv = [Tl(sb(nc, es, "wkv%d" % i, [128, 2, 256], BF16)[:, :, :]) for i in range(2)]
        kTn = [sb(nc, es, "kTn%d" % i, [128, S], BF16) for i in range(2)]
        vH = [sb(nc, es, "vH%d" % i, [128, NT, 128], BF16) for i in range(2)]
        qn = [sb(nc, es, "qn%d" % i, [128, S], BF16) for i in range(2)]
        qr = [sb(nc, es, "qr%d" % i, [128, S], BF16) for i in range(2)]
        kTn_t = [[Tl(kTn[i][:, b * 512:(b + 1) * 512]) for b in range(8)] for i in range(2)]
        vH_t = [[Tl(vH[i][:, 4 * b:4 * b + 4, :]) for b in range(8)] for i in range(2)]
        qn_t = [[Tl(qn[i][:, b * 512:(b + 1) * 512]) for b in range(8)] for i in range(2)]
        qr_t = [[Tl(qr[i][:, b * 512:(b + 1) * 512]) for b in range(8)] for i in range(2)]
        for i in range(2):
            P.v("pool", "memset", [], qr_t[i], qr[i][64:128, :], 0.0)
        pT = [Tl(sb(nc, es, "pT%d" % i, [128, 512], BF16)[:, :]) for i in range(6)]
        t1 = Tl(sb(nc, es, "t1" + tag, [64, 512], F32)[:, :])
        t2 = Tl(sb(nc, es, "t2" + tag, [64, 512], F32)[:, :])
        rr = [Tl(sb(nc, es, "rr%d" % i, [128, 512], F32)[:, :]) for i in range(2)]
        oo = [Tl(sb(nc, es, "oo%d" % i, [128, 512], BF16)[:, :]) for i in range(2)]
        sc = [Tl(ps(nc, es, "sc%d" % i, [128, 512], F32)[:, :]) for i in range(3)]
        oacc = [Tl(ps(nc, es, "oacc%d" % i, [128, 512], F32)[:, :]) for i in range(2)]
        sacc = [Tl(sb(nc, es, "sacc%d" % i, [128, 512], F32)[:, :]) for i in range(2)]
        saccP = [Tl(sb(nc, es, "saccP%d" % i, [128, 512], F32)[:, :]) for i in range(2)]
        sm = Tl(ps(nc, es, "sm", [128, 512], F32)[:, :])
        pj = [Tl(ps(nc, es, "pj%d" % i, [128, 512], F32)[:, :]) for i in range(2)]
        pjc = [0]

        def prep_items(h):
            sl = h % 2
            items = []

            def load_w():
                P.dma("pool", wq[sl].ap, uq_v[:, :, h, :], [], [wq[sl]])
                P.dma("pool", wqs[sl].ap[:, :, 0:32], uq_v[:, :, h, NOPE + 32:NOPE + 64], [], [wqs[sl]])
                P.dma("pool", wqs[sl].ap[:, :, 32:64], uq_v[:, :, h, NOPE:NOPE + 32], [], [wqs[sl]])
                P.dma("pool", wkv[sl].ap, up_v[:, :, h, :], [], [wkv[sl]])

            def nxt():
                a = pj[pjc[0] % len(pj)]
                pjc[0] += 1
                return a

            def k_item(b):
                def f():
                    acc = nxt()
                    for c in range(2):
                        P.mm(acc.ap, wkv[sl].ap[:, c, 0:128], ckvs[:, c, b * 512:(b + 1) * 512], c == 0, c == 1, [wkv[sl], ckv_t[b]], [acc])
                    P.act(kTn_t[sl][b].ap, acc.ap, AF.Copy, [acc], [kTn_t[sl][b]])
                return f

            def v_item(b):
                def f():
                    acc = nxt()
                    for t in range(4):
                        ti = 4 * b + t
                        for c in range(2):
                            P.mm(acc.ap[:, t * 128:(t + 1) * 128], ckvs[:, c, ti * 128:(ti + 1) * 128], wkv[sl].ap[:, c, 128:256],
                                 c == 0, c == 1, [wkv[sl], ckv_t[b]], [acc])
                    P.act(vH_t[sl][b].ap, acc.ap.rearrange("p (t d) -> p t d", t=4), AF.Copy, [acc], [vH_t[sl][b]])
                return f

            def qn_item(b):
                def f():
                    acc = nxt()
                    for c in range(3):
                        P.mm(acc.ap, wq[sl].ap[:, c, 0:128], cqs[:, c, b * 512:(b + 1) * 512], c == 0, c == 2, [wq[sl], cq_t[b]], [acc])
                    P.act(qn_t[sl][b].ap, acc.ap, AF.Copy, [acc], [qn_t[sl][b]])
                return f

            def qr_item(b):
                def f():
                    cs = slice(b * 512, (b + 1) * 512)
                    a1 = nxt()
                    for c in range(3):
                        P.mm(a1.ap[0:64, :], wq[sl].ap[:, c, 128:192], cqs[:, c, cs], c == 0, c == 2, [wq[sl], cq_t[b]], [a1])
                    P.v("dve", "tensor_tensor", [a1, C.cosT], [t1], t1.ap, a1.ap[0:64, :], C.cosT.ap[:, cs], ALU.mult)
                    a2 = nxt()
                    for c in range(3):
                        P.mm(a2.ap[0:64, :], wqs[sl].ap[:, c, :], cqs[:, c, cs], c == 0, c == 2, [wqs[sl], cq_t[b]], [a2])
                    P.v("dve", "tensor_tensor", [a2, C.sinT], [t2], t2.ap, a2.ap[0:64, :], C.sinT.ap[:, cs], ALU.mult)
                    P.v("dve", "tensor_tensor", [t1, t2], [qr_t[sl][b]], qr[sl][0:64, b * 512:(b + 1) * 512], t1.ap, t2.ap, ALU.add)
                return f

            items.append(load_w)
            for b in range(8):
                items += [k_item(b), v_item(b), qn_item(b), qr_item(b)]
            return items

        def stageA(i, t):
            h, b, kt = t["u"], t["b"], t["kt"]
            sl = h % 2
            di, c0 = diag_range(kt, b)
            qs = slice(b * 512 + c0, (b + 1) * 512)
            ks = slice(kt * 128, (kt + 1) * 128)
            s_ = sc[i % len(sc)]
            p_ = pT[i % len(pT)]
            P.mm(s_.ap[:, c0:512], kTn[sl][:, ks], qn[sl][:, qs], True, False, [kTn_t[sl][kt // 4], qn_t[sl][b]], [s_])
            P.mm(s_.ap[:, c0:512], krts[:, ks], qr[sl][:, qs], False, True, [krt_t[kt // 4], qr_t[sl][b]], [s_])
            P.act(p_.ap[:, c0:512], s_.ap[:, c0:512], AF.Exp, [s_], [p_], scale=scale)
            if di >= 0:
                P.v("pool", "tensor_tensor", [p_, C.tri], [p_], p_.ap[:, c0:c0 + 128], p_.ap[:, c0:c0 + 128], C.tri.ap, ALU.mult)

        def stageB(i, t):
            h, b, kt, pc = t["u"], t["b"], t["kt"], t["pc"]
            sl = h % 2
            di, c0 = diag_range(kt, b)
            p_ = pT[i % len(pT)]
            oa = oacc[pc % len(oacc)]
            sa = sacc[pc % 2]
            nk = 4 * b + 4
            P.mm(oa.ap[:, c0:512], vH[sl][:, kt, :], p_.ap[:, c0:512], kt == 0, kt == nk - 1, [vH_t[sl][kt // 4], p_], [oa])
            sp_ = saccP[pc % 2]
            if kt == 0:
                P.v("dve", "tensor_copy", [p_], [sa], sa.ap, p_.ap)
                P.v("pool", "memset", [], [sp_], sp_.ap, 0.0)
            elif kt % 3 == 1:
                P.v("pool", "tensor_tensor", [sp_, p_], [sp_], sp_.ap[:, c0:512], sp_.ap[:, c0:512], p_.ap[:, c0:512], ALU.add)
            else:
                P.v("dve", "tensor_tensor", [sa, p_], [sa], sa.ap[:, c0:512], sa.ap[:, c0:512], p_.ap[:, c0:512], ALU.add)

        def epilogue(t):
            h, b, pc = t["u"], t["b"], t["pc"]
            oa = oacc[pc % len(oacc)]
            sa = sacc[pc % 2]
            r_ = rr[pc % 2]
            o_ = oo[pc % 2]
            sp_ = saccP[pc % 2]

            def late():
                P.mm(sm.ap, C.onesf.ap, sa.ap, True, False, [C.onesf, sa], [sm])
                P.mm(sm.ap, C.onesf.ap, sp_.ap, False, True, [C.onesf, sp_], [sm])
                P.act(r_.ap, sm.ap, AF.Ln, [sm], [r_])
                P.act(r_.ap, r_.ap, AF.Exp, [r_], [r_], scale=-1.0)
                P.v("dve", "tensor_tensor", [oa, r_], [o_], o_.ap, oa.ap, r_.ap, ALU.mult)
                P.dma("sp", OT[h, :, b * 512:(b + 1) * 512], o_.ap, [o_], [OT_t[h][b]])
            return late

        tiles = []
        pc = 0
        for h in range(MLA_H):
            for b in range(8):
                for kt in range(4 * b + 4):
                    tiles.append(dict(u=h, b=b, kt=kt, pc=pc, last=(kt == 4 * b + 3), bi=b))
                pc += 1
        drive_attention(tiles, ATT_LA, stageA, stageB, epilogue, prep_items, MLA_H, 8, defer=MLA_DEFER)
        P.barrier()


def build_program(stages=99, debug=False, first=0, part=None):
    nc = bass.Bass("TRN2", target_bir_lowering=False)
    dbg_kind = "ExternalOutput" if debug else "Internal"
    if part == 0:
        first, stages = 0, 6
    elif part == 1:
        first, stages = 6, 12

    def inp(name, shape, dt=F32):
        return nc.dram_tensor(name, list(shape), dt, kind="ExternalInput").ap()

    x = inp("x", [S, D])
    p = inp("p", [DEPTH, S, PLE])
    ffn_w_in = [inp("ffn1_w_in", [DEPTH, D, 2 * DFF]), inp("ffn2_w_in", [DEPTH, D, 2 * DFF])]
    ffn_w_out = [inp("ffn1_w_out", [DEPTH, DFF, D]), inp("ffn2_w_out", [DEPTH, DFF, D])]
    ln_g = inp("ln_g", [DEPTH * 4, D])
    ln_b = inp("ln_b", [DEPTH * 4, D])
    ple_w_gate = inp("ple_w_gate", [DEPTH, D, D])
    ple_b_gate = inp("ple_b_gate", [DEPTH, D])
    ple_w_proj = inp("ple_w_proj", [DEPTH, PLE, D])
    fox_w_in = inp("fox_w_in", [D, 3 * D + FOX_H])
    fox_b_f = inp("fox_b_f", [1, FOX_H])
    fox_w_o = inp("fox_w_o", [D, D])
    mla_w_dq = inp("mla_w_dq", [D, QL])
    mla_q_norm = inp("mla_q_norm", [1, QL])
    mla_w_uq = inp("mla_w_uq", [QL, MLA_H, NOPE + ROPE])
    mla_w_o = inp("mla_w_o", [MLA_H * VD, D])
    kv_w_down = inp("kv_w_down", [D, KVL + ROPE])
    kv_norm = inp("kv_norm", [1, KVL])
    kv_w_up = inp("kv_w_up", [KVL, MLA_H, NOPE + VD])
    c_ident = inp("c_ident", [128, 128], BF16)
    c_tri = inp("c_tri", [128, 128], BF16)
    c_Lf = inp("c_Lf", [128, 128])
    c_E127f = inp("c_E127f", [128, 128])
    c_E127b = inp("c_E127b", [128, 128], BF16)
    c_sel = inp("c_sel", [16, 16, 128], BF16)
    c_onesf = inp("c_onesf", [128, 128])
    c_onesb = inp("c_onesb", [128, 128], BF16)
    c_cosT = inp("c_cosT", [64, S])
    c_sinT = inp("c_sinT", [64, S])

    hand = {None: dbg_kind, 0: "ExternalOutput", 1: "ExternalInput"}[part]
    out = nc.dram_tensor("out", [S, D], F32, kind="ExternalOutput" if part != 0 else "Internal").ap()
    XA = nc.dram_tensor("XA", [S, D], F32, kind=dbg_kind).ap()
    XB = nc.dram_tensor("XB", [S, D], F32, kind=hand).ap()
    OT0 = nc.dram_tensor("OT0", [16, 64, S], BF16, kind=dbg_kind).ap()
    OT1 = nc.dram_tensor("OT1", [16, 128, S], BF16, kind=dbg_kind).ap()
    CKVT = nc.dram_tensor("CKVT", [2, 128, S], BF16, kind=hand).ap()
    KRT = nc.dram_tensor("KRT", [64, S], BF16, kind=hand).ap()
    CQT = nc.dram_tensor("CQT", [3, 128, S], BF16, kind=dbg_kind).ap()

    def tiles(ap):
        return [Tl(ap[i * 128:(i + 1) * 128, :]) for i in range(NT)]

    Xx, XAt, XBt, Outt = tiles(x), tiles(XA), tiles(XB), tiles(out)
    Pt = [[Tl(p[l, i * 128:(i + 1) * 128, :]) for i in range(NT)] for l in range(DEPTH)]
    OT0_t = [[Tl() for b in range(8)] for h in range(16)]
    OT1_t = [[Tl() for b in range(8)] for h in range(16)]
    CKVT_t = [Tl() for b in range(8)]
    KRT_t = [Tl() for b in range(8)]
    CQT_t = [Tl() for b in range(8)]

    with ExitStack() as es:
        P = Prog(nc, es)
        C = Ctx()

        C.ident = load_const(P, es, "k_ident", c_ident, [128, 128], BF16)
        C.d = dict(tri=(c_tri, [128, 128], BF16), Lf=(c_Lf, [128, 128], F32), E127f=(c_E127f, [128, 128], F32),
                   E127b=(c_E127b, [128, 128], BF16), sel=(c_sel, [16, 16, 128], BF16), onesf=(c_onesf, [128, 128], F32),
                   onesb=(c_onesb, [128, 128], BF16), cosT=(c_cosT, [64, S], F32), sinT=(c_sinT, [64, S], F32))
        C.eps_ln = Tl(sb(nc, es, "eps_ln", [128, 1], F32)[:, :])
        P.v("dve", "memset", [], [C.eps_ln], C.eps_ln.ap, LN_EPS / (ALPHA * ALPHA))

        def lng(l, i):
            return ln_g[4 * l + i:4 * l + i + 1, :], ln_b[4 * l + i:4 * l + i + 1, :]

        seq = []
        seq.append(lambda last: phase_ffn(P, C, Xx, Outt if last else XAt, ffn_w_in[0][0], ffn_w_out[0][0], *lng(0, 0), "a"))
        seq.append(lambda last: phase_fox(P, C, XAt, fox_w_in, fox_b_f, OT0, OT0_t))
        seq.append(lambda last: phase_attout(P, C, XAt, Outt if last else XBt, OT0.rearrange("(g e) d t -> g (e d) t", e=2),
                                             lambda c, b: [OT0_t[2 * c][b], OT0_t[2 * c + 1][b]], 8, fox_w_o, *lng(0, 1), "b"))
        seq.append(lambda last: phase_ffn(P, C, XBt, Outt if last else XAt, ffn_w_in[1][0], ffn_w_out[1][0], *lng(0, 2), "c", pre=PRE[0]))
        seq.append(lambda last: phase_ple(P, C, XAt, Outt if last else XBt, Pt[0], ple_w_gate[0], ple_b_gate[0:1, :], ple_w_proj[0],
                                          *lng(0, 3), "d"))
        seq.append(lambda last: phase_latent(P, C, XBt, kv_w_down, KVL, kv_norm, CKVT, CKVT_t, "e",
                                             rope_w_d=kv_w_down, KRT=KRT, KRT_t=KRT_t))
        seq.append(lambda last: phase_ffn(P, C, XBt, Outt if last else XAt, ffn_w_in[0][1], ffn_w_out[0][1], *lng(1, 0), "f", pre=PRE[0]))
        seq.append(lambda last: phase_latent(P, C, XAt, mla_w_dq, QL, mla_q_norm, CQT, CQT_t, "g"))
        seq.append(lambda last: phase_mla(P, C, CQT, CQT_t, CKVT, CKVT_t, KRT, KRT_t, mla_w_uq, kv_w_up, OT1, OT1_t))
        seq.append(lambda last: phase_attout(P, C, XAt, Outt if last else XBt, OT1, lambda c, b: [OT1_t[c][b]], 16, mla_w_o,
                                             *lng(1, 1), "h"))
        seq.append(lambda last: phase_ffn(P, C, XBt, Outt if last else XAt, ffn_w_in[1][1], ffn_w_out[1][1], *lng(1, 2), "i", pre=PRE[0]))
        seq.append(lambda last: phase_ple(P, C, XAt, Outt, Pt[1], ple_w_gate[1], ple_b_gate[1:2, :], ple_w_proj[1], *lng(1, 3), "j"))
        n = min(stages, len(seq))
        ffn_w = {3: (ffn_w_in[1][0], "c"), 6: (ffn_w_in[0][1], "f"), 10: (ffn_w_in[1][1], "i")}
        i = first
        while i < n:
            if i + 1 in ffn_w and i + 1 < n and PREFETCH:
                with ExitStack() as pf:
                    PRE[0] = ffn_prefetch(P, pf, *ffn_w[i + 1])
                    seq[i](False)
                    seq[i + 1](i + 1 == n - 1 and part != 0)
                    PRE[0] = None
                i += 2
            else:
                seq[i](i == n - 1 and part != 0)
                i += 1
        P.barrier()
    return nc


def host_consts():
    import ml_dtypes
    bf = ml_dtypes.bfloat16
    idx = np.arange(128)
    tri = (idx[:, None] <= idx[None, :]).astype(np.float32)
    e127 = np.zeros((128, 128), np.float32)
    e127[127, :] = 1.0
    sel = np.zeros((16, 16, 128), np.float32)
    for h in range(16):
        sel[h, h, :] = 1.0
    half = ROPE // 2
    inv = (np.float32(10000.0) ** (-np.arange(half, dtype=np.float32) * np.float32(2.0 / ROPE))).astype(np.float32)
    ang = (np.arange(S, dtype=np.float32)[:, None] * inv[None, :]).astype(np.float32)
    cos = np.cos(ang).astype(np.float32).T
    sin = np.sin(ang).astype(np.float32).T
    return {
        "c_ident": np.eye(128, dtype=np.float32).astype(bf),
        "c_tri": tri.astype(bf),
        "c_Lf": tri,
        "c_E127f": e127,
        "c_E127b": e127.astype(bf),
        "c_sel": sel.astype(bf),
        "c_onesf": np.ones((128, 128), np.float32),
        "c_onesb": np.ones((128, 128), np.float32).astype(bf),
        "c_cosT": np.ascontiguousarray(np.concatenate([cos, cos], 0)),
        "c_sinT": np.ascontiguousarray(np.concatenate([-sin, sin], 0)),
    }


def make_in_maps(inputs, cores=range(N_CORES)):
    consts = host_consts()
    f = lambda a: np.ascontiguousarray(np.asarray(a, dtype=np.float32))
    shared = {
        "ln_g": f(inputs["ln_g"]).reshape(DEPTH * 4, D), "ln_b": f(inputs["ln_b"]).reshape(DEPTH * 4, D),
        "fox_w_in": f(inputs["fox_w_in"][0]), "fox_b_f": f(inputs["fox_b_f"]).reshape(1, FOX_H), "fox_w_o": f(inputs["fox_w_o"][0]),
        "mla_w_dq": f(inputs["mla_w_dq"][0]), "mla_q_norm": f(inputs["mla_q_norm"]).reshape(1, QL),
        "mla_w_uq": f(inputs["mla_w_uq"][0]), "mla_w_o": f(inputs["mla_w_o"][0]),
        "kv_w_down": f(inputs["kv_w_down"]), "kv_norm": f(inputs["kv_norm"]).reshape(1, KVL), "kv_w_up": f(inputs["kv_w_up"]),
    }
    for k in ("ffn1_w_in", "ffn1_w_out", "ffn2_w_in", "ffn2_w_out", "ple_w_gate", "ple_b_gate", "ple_w_proj"):
        shared[k] = f(inputs[k])
    maps = []
    for c in cores:
        m = dict(consts)
        m.update(shared)
        m["x"] = f(inputs["x"][c])
        m["p"] = f(inputs["p"][:, c])
        maps.append(m)
    return maps


FUSED = True


def kernel(**inputs):
    inputs = {k: np.asarray(v) for k, v in inputs.items()}
    maps = make_in_maps(inputs)
    cores = list(range(N_CORES))
    if FUSED:
        res = run_bass_kernel_spmd(build_program(), maps, core_ids=cores)
    else:
        ra = run_bass_kernel_spmd(build_program(part=0), maps, core_ids=cores)
        for m, r in zip(maps, ra.results):
            for k in ("XB", "CKVT", "KRT"):
                m[k] = np.asarray(r[k])
        res = run_bass_kernel_spmd(build_program(part=1), maps, core_ids=cores)
    return np.stack([np.asarray(r["out"]) for r in res.results], axis=0).astype(np.float32)
```
